# Optimizing a Trainium2 kernel written in Bass

```python
import jax, jax.numpy as jnp
from jax import lax
import numpy as np

D_MODEL = 1024
BATCH = 8
SEQ = 8192
DEPTH = 1
DEC_BATCH = 8
DEC_SEQ = 2048
PAST_LEN = 128

GRID_W = 64
POOL_WIDTH = D_MODEL // 2
POOL_GROUPS = 4
POOL_GROUP_W = POOL_WIDTH // POOL_GROUPS
POOL_WINDOWS = (2, 4, 8, 16)
NA_HEADS = 8
NA_HEAD_DIM = 64
NA_WIDTH = NA_HEADS * NA_HEAD_DIM
NA_ROWS_MAX = 8
NA_COLS = 16
NA_KEY_COLS = 2 * NA_COLS
N_COL_BLOCKS = GRID_W // NA_COLS
D_FF = 4 * D_MODEL
N_BRANCH = 2
IN_WIDTH = POOL_WIDTH + 3 * NA_WIDTH + N_BRANCH * D_MODEL
RMS_EPS = 1e-6
NEG_INF = -1e30

kernel_name = "hybrid_pool_natten_encoder"


def _rmsnorm(x, g):
    xf = x.astype(jnp.float32)
    var = jnp.mean(xf * xf, axis=-1, keepdims=True)
    return (xf * lax.rsqrt(var + RMS_EPS) * g.astype(jnp.float32)).astype(x.dtype)


def _multiscale_pool(p, w_grp, scale):
    b, s, _ = p.shape
    pf = p.astype(jnp.float32)
    csum = jnp.concatenate([jnp.zeros((b, 1, POOL_WIDTH), jnp.float32), jnp.cumsum(pf, axis=1)], axis=1)
    t = np.arange(s)
    outs = []
    for g, w in enumerate(POOL_WINDOWS):
        lo = np.clip(t - w // 2, 0, s)
        hi = np.clip(t + w // 2, 0, s)
        cnt = jnp.asarray(hi - lo, jnp.float32)[None, :, None]
        sl = slice(g * POOL_GROUP_W, (g + 1) * POOL_GROUP_W)
        cg = csum[..., sl]
        mean = (jnp.take(cg, jnp.asarray(hi), axis=1) - jnp.take(cg, jnp.asarray(lo), axis=1)) / cnt
        outs.append(mean - pf[..., sl])
    pooled = jnp.stack(outs, axis=2).astype(p.dtype)
    mixed = jnp.einsum('bsgc,gcd->bsgd', pooled, w_grp).reshape(b, s, POOL_WIDTH)
    return mixed * scale


def _na_tables(rows):
    kr = min(NA_ROWS_MAX, rows)
    r = np.arange(rows)
    rs = np.clip(r - kr // 2, 0, rows - kr)
    row_idx = rs[:, None] + np.arange(kr)[None, :]
    dr = row_idx - r[:, None]
    c0 = np.arange(N_COL_BLOCKS) * NA_COLS
    kcs = np.clip(c0 - NA_COLS // 2, 0, GRID_W - NA_KEY_COLS)
    col_idx = kcs[:, None] + np.arange(NA_KEY_COLS)[None, :]
    qcol = c0[:, None] + np.arange(NA_COLS)[None, :]
    cs = np.clip(qcol - NA_COLS // 2, 0, GRID_W - NA_COLS)
    kc = col_idx[:, None, :]
    dc = kc - qcol[:, :, None]
    col_valid = (kc >= cs[..., None]) & (kc < cs[..., None] + NA_COLS)
    return kr, row_idx, dr, col_idx, dc, col_valid


def _neighbourhood_attention(q, k, v, rpb):
    b, s, _ = q.shape
    rows = s // GRID_W
    kr, row_idx, dr, col_idx, dc, col_valid = _na_tables(rows)
    dr_i = jnp.asarray(dr + NA_ROWS_MAX - 1)[:, None, None, :, None]
    dc_i = jnp.asarray(np.clip(dc + NA_COLS - 1, 0, 2 * NA_COLS - 2))[None, :, :, None, :]
    bias = rpb.astype(jnp.float32)[:, dr_i, dc_i]
    bias = jnp.where(jnp.asarray(col_valid)[None, None, :, :, None, :], bias, NEG_INF)
    row_idx_j = jnp.asarray(row_idx)
    col_idx_j = jnp.asarray(col_idx)
    scale = NA_HEAD_DIM ** -0.5
    grid = (b, rows, GRID_W, NA_HEADS, NA_HEAD_DIM)

    def one(args):
        qe, ke, ve = args
        qb = qe.reshape(rows, N_COL_BLOCKS, NA_COLS, NA_HEADS, NA_HEAD_DIM)
        kb = ke[row_idx_j][:, :, col_idx_j]
        vb = ve[row_idx_j][:, :, col_idx_j]
        sc = jnp.einsum('rnqhd,rinjhd->hrnqij', qb, kb,
                        preferred_element_type=jnp.float32) * scale + bias
        sh = sc.shape
        pr = jax.nn.softmax(sc.reshape(sh[:4] + (kr * NA_KEY_COLS,)), axis=-1).reshape(sh)
        o = jnp.einsum('hrnqij,rinjhd->rnqhd', pr.astype(ve.dtype), vb)
        return o.reshape(s, NA_WIDTH)

    return lax.map(one, (q.reshape(grid), k.reshape(grid), v.reshape(grid)))


def _layer(x, norm_mix, w_in, b_gate, w_pool_grp, pool_scale, w_pool_proj, rpb,
           w_na_proj, w_out, norm_mlp, w_ff1, w_ff2):
    xn = _rmsnorm(x, norm_mix)
    z = xn @ w_in
    o1 = POOL_WIDTH
    o2 = o1 + NA_WIDTH
    o3 = o2 + NA_WIDTH
    o4 = o3 + NA_WIDTH
    p, q, k, v, g = z[..., :o1], z[..., o1:o2], z[..., o2:o3], z[..., o3:o4], z[..., o4:]
    gates = jax.nn.sigmoid(g + b_gate)
    g_pool, g_na = gates[..., :D_MODEL], gates[..., D_MODEL:]
    pool_out = _multiscale_pool(p, w_pool_grp, pool_scale) @ w_pool_proj
    na_out = _neighbourhood_attention(q, k, v, rpb) @ w_na_proj
    x = x + (g_pool * pool_out + g_na * na_out) @ w_out
    xn = _rmsnorm(x, norm_mlp)
    hid = jnp.square(jax.nn.relu(xn @ w_ff1))
    return x + hid @ w_ff2


def _trunk(x, norm_mix, w_in, b_gate, w_pool_grp, pool_scale, w_pool_proj, rpb,
           w_na_proj, w_out, norm_mlp, w_ff1, w_ff2, norm_final):
    for l in range(DEPTH):
        x = _layer(x, norm_mix[l], w_in[l], b_gate[l], w_pool_grp[l], pool_scale[l],
                   w_pool_proj[l], rpb[l], w_na_proj[l], w_out[l], norm_mlp[l],
                   w_ff1[l], w_ff2[l])
    return _rmsnorm(x, norm_final)


def setup_inputs(seed: int = 0) -> dict:
    key = jax.random.key(seed)
    ks = jax.random.split(key, 16)
    f32 = jnp.float32
    nrm = lambda k, shape, s: jax.random.normal(k, shape, f32) * s
    return {
        "x_prompt": nrm(ks[0], (BATCH, SEQ, D_MODEL), 1.0),
        "x_sample": nrm(ks[1], (DEC_BATCH, DEC_SEQ, D_MODEL), 1.0),
        "norm_mix": 1.0 + nrm(ks[2], (DEPTH, D_MODEL), 0.02),
        "w_in": nrm(ks[3], (DEPTH, D_MODEL, IN_WIDTH), D_MODEL ** -0.5),
        "b_gate": nrm(ks[4], (DEPTH, N_BRANCH * D_MODEL), 0.1),
        "w_pool_grp": nrm(ks[5], (DEPTH, POOL_GROUPS, POOL_GROUP_W, POOL_GROUP_W), POOL_GROUP_W ** -0.5),
        "pool_scale": 1.0 + nrm(ks[6], (DEPTH, POOL_WIDTH), 0.02),
        "w_pool_proj": nrm(ks[7], (DEPTH, POOL_WIDTH, D_MODEL), POOL_WIDTH ** -0.5),
        "rpb": nrm(ks[8], (DEPTH, NA_HEADS, 2 * NA_ROWS_MAX - 1, 2 * NA_COLS - 1), 0.1),
        "w_na_proj": nrm(ks[9], (DEPTH, NA_WIDTH, D_MODEL), NA_WIDTH ** -0.5),
        "w_out": nrm(ks[10], (DEPTH, D_MODEL, D_MODEL), D_MODEL ** -0.5),
        "norm_mlp": 1.0 + nrm(ks[11], (DEPTH, D_MODEL), 0.02),
        "w_ff1": nrm(ks[12], (DEPTH, D_MODEL, D_FF), D_MODEL ** -0.5),
        "w_ff2": nrm(ks[13], (DEPTH, D_FF, D_MODEL), D_FF ** -0.5),
        "norm_final": 1.0 + nrm(ks[14], (D_MODEL,), 0.02),
    }


def reference(x_prompt, x_sample, norm_mix, w_in, b_gate, w_pool_grp, pool_scale, w_pool_proj,
              rpb, w_na_proj, w_out, norm_mlp, w_ff1, w_ff2, norm_final):
    y_prompt = _trunk(x_prompt, norm_mix, w_in, b_gate, w_pool_grp, pool_scale, w_pool_proj, rpb,
                      w_na_proj, w_out, norm_mlp, w_ff1, w_ff2, norm_final)
    y_sample = _trunk(x_sample, norm_mix, w_in, b_gate, w_pool_grp, pool_scale, w_pool_proj, rpb,
                      w_na_proj, w_out, norm_mlp, w_ff1, w_ff2, norm_final)
    return (y_prompt, y_sample)
```

```python
import contextlib
import numpy as np
import concourse.bass as bass
import concourse.mybir as mybir
from concourse.bass_utils import run_bass_kernel_spmd

F32 = mybir.dt.float32
BF16 = mybir.dt.bfloat16
ALU = mybir.AluOpType
AF = mybir.ActivationFunctionType

D = 1024
NB = 5
NEG = -240000.0
EPS = 1e-6
POOL_W = (2, 4, 8, 16)

BK, BV, BP, BQ = 0, 1, 2, 3
BG = (4, 5, 6, 7)
BPP, BNP = 8, 9
BWO = (10, 11)
NBLK = 28


class Prog:
    def __init__(self, nc):
        self.nc = nc
        self.stack = contextlib.ExitStack()
        self.eng = {}
        self.nsem = 0
        for name in ("pe", "act", "dve", "pool", "sp"):
            self.eng[name] = dict(ops=[], sem=self.new_sem("e_" + name), tick=0, known={}, snap=None)
        self.lastw = {}
        self.readers = {}
        self.dsems = {}

    def new_sem(self, name):
        self.nsem += 1
        return self.stack.enter_context(self.nc.semaphore(name))

    def sb(self, name, shape, dtype):
        return self.stack.enter_context(self.nc.sbuf_tensor(name, shape, dtype))

    def op(self, eng, fn, reads=(), writes=(), dsem=None):
        E = self.eng[eng]
        known = E["known"]
        waits = {}

        def need(tok):
            if tok is None:
                return
            s, v, clk = tok
            sid = id(s)
            if eng == "pe" and s is E["sem"]:
                return
            if known.get(sid, 0) >= v:
                return
            if waits.get(sid, (None, 0, None))[1] < v:
                waits[sid] = tok

        for k in reads:
            need(self.lastw.get(k))
        for k in writes:
            need(self.lastw.get(k))
            for tok in self.readers.get(k, {}).values():
                need(tok)
        if waits:
            for sid, (s, v, clk) in waits.items():
                if known.get(sid, 0) < v:
                    known[sid] = v
                if clk is not None:
                    for k2, v2 in clk.items():
                        if known.get(k2, 0) < v2:
                            known[k2] = v2
            E["snap"] = None
        if E["snap"] is None:
            E["snap"] = dict(known)
        if dsem is None:
            E["tick"] += 1
            tok = (E["sem"], E["tick"], E["snap"])
            amount = 1
        else:
            if dsem not in self.dsems:
                self.dsems[dsem] = [self.new_sem("d_" + dsem), 0]
            ds = self.dsems[dsem]
            ds[1] += 16
            tok = (ds[0], ds[1], E["snap"])
            amount = 16
        E["ops"].append(([(s, v) for (s, v, c) in waits.values()], fn, tok[0], amount))
        sid = id(tok[0])
        for k in reads:
            self.readers.setdefault(k, {})[sid] = tok
        for k in writes:
            self.lastw[k] = tok
            self.readers[k] = {}
        return tok

    def replay(self, name, e):
        for waits, fn, sem, amount in self.eng[name]["ops"]:
            for (s, v) in waits:
                e.wait_ge(s, v)
            ins = fn(e)
            if ins is not None:
                ins.then_inc(sem, amount)


def _pool_mats():
    A = np.zeros((128, 20, 128), np.float32)
    for g, w in enumerate(POOL_W):
        h = w // 2
        for tl in range(128):
            for off in range(-h, h):
                src = tl + off
                val = 1.0 / w
                if src < 0:
                    A[128 + src, g * 5 + 0, tl] += val
                elif src >= 128:
                    A[src - 128, g * 5 + 2, tl] += val
                else:
                    A[src, g * 5 + 1, tl] += val
            A[tl, g * 5 + 1, tl] -= 1.0
            lo, hi = max(tl - h, 0), tl + h
            for src in range(lo, min(hi, 128)):
                A[src, g * 5 + 3, tl] += 1.0 / (hi - lo)
            A[tl, g * 5 + 3, tl] -= 1.0
            lo, hi = tl - h, min(tl + h, 128)
            for src in range(max(lo, 0), hi):
                A[src, g * 5 + 4, tl] += 1.0 / (hi - lo)
            A[tl, g * 5 + 4, tl] -= 1.0
    return A


def _col_masks():
    m = np.zeros((128, 64), np.float32)
    for a in range(2):
        for kc in range(64):
            for qc in range(64):
                cs = min(max(qc - 8, 0), 48)
                if cs <= kc < cs + 16:
                    m[a * 64 + kc, qc] = 1.0
    return m, (m - 1.0) * (-NEG)


def build_nc(seq_lens):
    nc = bass.Bass("TRN2", target_bir_lowering=False)
    P = Prog(nc)
    dram = {}

    def din(name, shape):
        dram[name] = nc.dram_tensor(name, list(shape), F32, kind="ExternalInput").ap()
        return dram[name]

    xin, yout = [], []
    for i, S in enumerate(seq_lens):
        xin.append(din("x%d" % i, (S, D)))
    for i, S in enumerate(seq_lens):
        yout.append(nc.dram_tensor("y%d" % i, [S, D], F32, kind="ExternalOutput").ap())
    norm_mix = din("norm_mix", (D,))
    w_in = din("w_in", (D, 4096))
    b_gate = din("b_gate", (2048,))
    w_grp = din("w_pool_grp", (4, 128, 128))
    pool_scale = din("pool_scale", (512,))
    w_pp = din("w_pool_proj", (512, D))
    rpb = din("rpb", (8, 15, 31))
    w_np = din("w_na_proj", (512, D))
    w_out = din("w_out", (D, D))
    norm_mlp = din("norm_mlp", (D,))
    w_ff1 = din("w_ff1", (D, 4096))
    w_ff2 = din("w_ff2", (4096, D))
    norm_final = din("norm_final", (D,))
    c_ident = din("c_ident", (128, 128))
    c_amat = din("c_amat", (128, 20, 128))
    c_m01 = din("c_m01", (128, 64))
    c_mneg = din("c_mneg", (128, 64))

    wb_in = nc.dram_tensor("wb_in", [D, 4096], BF16, kind="Internal").ap()
    wb_pp = nc.dram_tensor("wb_pp", [512, D], BF16, kind="Internal").ap()
    wb_np = nc.dram_tensor("wb_np", [512, D], BF16, kind="Internal").ap()
    wb_out = nc.dram_tensor("wb_out", [D, D], BF16, kind="Internal").ap()
    wb_f1 = nc.dram_tensor("wb_f1", [D, 4096], BF16, kind="Internal").ap()
    wb_f2 = nc.dram_tensor("wb_f2", [4096, D], BF16, kind="Internal").ap()
    rpbp = nc.dram_tensor("rpbp", [8, 15, 160], F32, kind="Internal").ap()
    bexp = nc.dram_tensor("bexp", [120, 64, 64], F32, kind="Internal").ap()

    ident = P.sb("ident", [128, 128], BF16)
    amat = P.sb("amat", [128, 20, 128], BF16)
    wgrp = P.sb("wgrp", [128, 4, 128], BF16)
    tab = P.sb("tab", [128, 8, 18, 64], BF16)
    m01 = P.sb("m01", [128, 64], F32)
    mneg = P.sb("mneg", [128, 64], F32)
    qbd = P.sb("qbd", [128, 4, 4, 2, 128], BF16)
    g1b = P.sb("g1b", [128, D], F32)
    g2b = P.sb("g2b", [128, D], F32)
    nfb = P.sb("nfb", [128, D], F32)
    bgT = P.sb("bgT", [128, 16], F32)
    hbT = P.sb("hbT", [128, 16], F32)
    psT = P.sb("psT", [128, 4], F32)
    nhalf = P.sb("nhalf", [128, 1], F32)
    ssq = P.sb("ssq", [128, 8], F32)
    var = P.sb("var", [128, 8], F32)
    rstd = P.sb("rstd", [128, 8], F32)
    rec = P.sb("rec", [128, 2, 8], F32)
    xa = P.sb("xa", [128, 2, D], F32)
    xs = P.sb("xs", [128, 2, D], BF16)
    xnT = P.sb("xnT", [128, 8, 1024], BF16)
    kT = P.sb("kT", [128, 4, 1024], BF16)
    vtok = P.sb("vtok", [128, 8, 8, 65], BF16)
    ptok = P.sb("ptok", [128, 8, 512], BF16)
    arena = P.sb("arena", [128, 12288], BF16)
    oT = arena[:, 0:2048].rearrange("p (c t) -> p c t", c=4)
    pooledT = arena[:, 2048:4096].rearrange("p (c t) -> p c t", c=4)
    mixedT = arena[:, 4096:6144].rearrange("p (c t) -> p c t", c=4)
    hidT = arena[:, 0:8192].rearrange("p (c t) -> p c t", c=16)
    mT = arena[:, 8192:12288].rearrange("p (c t) -> p c t", c=8)
    HID_OT = ["hid%d" % c for c in range(0, 4)]
    HID_PO = ["hid%d" % c for c in range(4, 8)]
    HID_MX = ["hid%d" % c for c in range(8, 12)]
    xn2T = P.sb("xn2T", [128, 8, 512], BF16)
    tg = P.sb("tg", [128, 2, 2, 512], F32)
    ug = P.sb("ug", [128, 2, 512], F32)
    rl = P.sb("rl", [128, 1, 512], F32)
    NPT = 5
    pt_ctr = [0]
    pTs = P.sb("pTs", [128, NPT, 512], BF16)
    otok = P.sb("otok", [128, 2, 512], BF16)
    xr = P.sb("xr", [128, 4, D], F32)
    wring = P.sb("wring", [128, NB, 4096], BF16)
    psum = P.stack.enter_context(nc.psum_tensor("psum", [128, 4096], F32))

    bank_ctr = [0]
    reserved = set()

    def nbank():
        while True:
            b = bank_ctr[0] % 8
            bank_ctr[0] += 1
            if b not in reserved:
                return b

    def pb(b, lo=0, hi=512):
        return psum[:, b * 512 + lo: b * 512 + hi]

    def pb16(b, n):
        return psum[:, b * 512: b * 512 + n // 2].bitcast(BF16)

    def cast(dst, src, key):
        P.op("pool", lambda e: e.dma_start(out=dst, in_=src, max_dma_last_dim=8192), writes=[key], dsem=key)

    cast(ident[:], c_ident, "ident")
    cast(amat[:], c_amat, "amat")
    cast(wgrp[:], w_grp.rearrange("g c d -> c g d"), "wgrp")
    cast(wb_in, w_in, "wb_in")
    pending_casts = {1: [(wb_pp, w_pp, "wb_pp"), (wb_np, w_np, "wb_np"), (wb_out, w_out, "wb_out")]}
    for pi in range(8):
        pending_casts.setdefault(2 + pi // 2, []).append((wb_f1[pi * 128:(pi + 1) * 128, :], w_ff1[pi * 128:(pi + 1) * 128, :], "wb_f1"))
    late_casts = []
    for pi in range(8):
        late_casts.append((wb_f2[pi * 512:(pi + 1) * 512, :], w_ff2[pi * 512:(pi + 1) * 512, :], "wb_f2_%d" % (pi // 2)))

    def kview(w, r0, c0, ncol=512):
        return w[r0:r0 + 1024, c0:c0 + ncol].rearrange("(kc p) n -> p kc n", p=128)

    def block_srcs(b):
        v8 = lambda t: t.rearrange("p (kc n) -> p kc n", kc=8)
        if b == BK:
            return [(v8, kview(wb_in, 0, 1024), "wb_in")]
        if b == BV:
            return [(v8, kview(wb_in, 0, 1536), "wb_in")]
        if b == BP:
            return [(v8, kview(wb_in, 0, 0), "wb_in")]
        if b == BQ:
            return [(v8, kview(wb_in, 0, 512), "wb_in")]
        if b in BG:
            bi = BG.index(b)
            out = []
            for br in range(2):
                c0 = 2048 + br * 1024 + 2 * bi * 128
                out.append((lambda t, br=br: t.rearrange("p (kc br n) -> p kc br n", kc=8, br=2)[:, :, br, :],
                            kview(wb_in, 0, c0, 256), "wb_in"))
            return out
        if b == BNP:
            return [(lambda t: t.rearrange("p (kc n) -> p kc n", kc=4), wb_np.rearrange("(kc p) n -> p kc n", p=128), "wb_np")]
        if b in BWO:
            return [(v8, kview(wb_out, 0, BWO.index(b) * 512), "wb_out")]
        half, r = (b - 12) // 8, (b - 12) % 8
        if r < 4:
            return [(v8, kview(wb_f1, 0, (half * 4 + r) * 512), "wb_f1")]
        ch, kg = (r - 4) // 2, (r - 4) % 2
        return [(v8, kview(wb_f2, (half * 2 + kg) * 1024, ch * 512), "wb_f2_%d" % (half * 2 + kg))]

    def load(dst, src, key, slow=False):
        P.op("sp", lambda e: e.dma_start(out=dst, in_=src, allow_slow_non_contiguous=slow), writes=[key], dsem=key)

    load(m01[:], c_m01, "m01")
    load(mneg[:], c_mneg, "mneg")
    load(g1b[:], norm_mix.partition_broadcast(128), "g1b")
    load(g2b[:], norm_mlp.partition_broadcast(128), "g2b")
    load(nfb[:], norm_final.partition_broadcast(128), "nfb")
    load(bgT[:], b_gate.rearrange("(c p) -> p c", p=128), "bgT", slow=True)
    load(psT[:], pool_scale.rearrange("(c p) -> p c", p=128), "psT", slow=True)
    P.op("dve", lambda e: e.memset(nhalf[:], -0.5), writes=["nhalf"])
    P.op("dve", lambda e: e.memset(qbd[:].rearrange("p a b c d -> p (a b c d)"), 0.0), writes=["qbd%d_%d" % (c, hh) for c in range(4) for hh in range(2)])
    P.op("dve", lambda e: e.tensor_scalar(out=hbT[:], in0=bgT[:], scalar1=0.5, scalar2=None, op0=ALU.mult),
         reads=["bgT"], writes=["hbT"])
    P.op("dve", lambda e: e.memset(vtok[:, :, :, 64:65], 1.0), writes=["vtok%d" % c for c in range(8)])

    xrf = xr[:].rearrange("p a b -> p (a b)")
    table_steps = []

    def traw():
        P.op("sp", lambda e: e.dma_start(out=xr[0:120, 1, 256:287], in_=rpb.rearrange("h a c -> (h a) c")),
             writes=["xr1"], dsem="rawrpb")

    def tstep0():
        pad = xr[0:120, 1, 0:160]
        P.op("dve", lambda e: e.memset(pad, 0.0), writes=["xr1p"])

        def frev(e):
            ins = None
            for i in range(31):
                ins = e.tensor_copy(out=xr[0:120, 1, 64 + i:65 + i], in_=xr[0:120, 1, 256 + 30 - i:257 + 30 - i])
            return ins
        P.op("dve", frev, reads=["xr1", "xr1p"], writes=["xr1p"])
        P.op("sp", lambda e: e.dma_start(out=rpbp.rearrange("h a c -> (h a) c"), in_=pad),
             reads=["xr1p"], writes=["rpbp"], dsem="rpbp")

    def tstep1():
        for h in range(8):
            src = bass.AP(tensor=rpbp.tensor, offset=h * 15 * 160 + 64 + 15, ap=[[160, 15], [-1, 64], [1, 64]])
            P.op("sp", lambda e, h=h, src=src: e.dma_start(out=bexp[h * 15:(h + 1) * 15], in_=src),
                 reads=["rpbp"], writes=["bexp"], dsem="bexp")

    stage = xrf[:, 0:3584].rearrange("p (h i q) -> p h i q", h=4, i=14)
    stf = xrf[:, 0:3584].rearrange("p (r q) -> p r q", q=64)

    def tstage(hh):
        for hl in range(4):
            h = hh * 4 + hl
            for a in range(2):
                src = bass.AP(tensor=bexp.tensor, offset=(h * 15 + a + 13) * 4096,
                              ap=[[64, 64], [-4096, 14], [1, 64]])
                P.op("sp", lambda e, a=a, hl=hl, src=src: e.dma_start(out=stage[a * 64:(a + 1) * 64, hl], in_=src),
                     reads=["bexp"], writes=["xr0"], dsem="stage")

    def tdve(hh):
        P.op("dve", lambda e: e.scalar_tensor_tensor(out=stf, in0=stf, scalar=8.0,
                                                     in1=m01[:].unsqueeze(1).broadcast_to([128, 56, 64]),
                                                     op0=ALU.mult, op1=ALU.mult),
             reads=["xr0", "m01"], writes=["xr0"])
        P.op("dve", lambda e: e.tensor_tensor(
            out=tab[:, hh * 4:(hh + 1) * 4, 0:14, :], in0=stage,
            in1=mneg[:].unsqueeze(1).unsqueeze(1).broadcast_to([128, 4, 14, 64]), op=ALU.add),
            reads=["xr0", "mneg"], writes=["tab"])

    def tfinal():
        P.op("dve", lambda e: e.memset(tab[:, :, 14, :], NEG), writes=["tab"])
        P.op("dve", lambda e: e.tensor_copy(out=tab[:, :, 15, :], in_=tab[:, :, 3, :]), reads=["tab"], writes=["tab"])
        P.op("dve", lambda e: e.memset(tab[64:128, :, 15, :], NEG), writes=["tab"])
        P.op("dve", lambda e: e.tensor_copy(out=tab[:, :, 16:18, :], in_=tab[:, :, 10:12, :]), reads=["tab"], writes=["tab"])
        P.op("dve", lambda e: e.memset(tab[0:64, :, 17, :], NEG), writes=["tab"])

    table_steps += [tstep0, tstep1, lambda: tstage(0), lambda: (tdve(0), tstage(1)), lambda: (tdve(1), tfinal())]

    order = []
    for S in seq_lens:
        order += [BK, BV, BP, BK, BV, BP, BK, BV, BP]
        for s in range(S // 512):
            order += [BQ, BG[0], BNP, BG[1], BG[2], BG[3], BWO[0], BWO[1]]
            if 2 * s + 3 < S // 256:
                order += [BK, BV, BP]
            for half in range(2):
                order += [12 + half * 8 + i for i in range(4)]
                order += [16 + half * 8 + i for i in range(4)]
    ws = dict(nload=0, nuse=0, free=list(range(NB)), slot_of={})

    def ws_fill():
        while ws["free"] and ws["nload"] < len(order):
            i = ws["nload"]
            ws["nload"] += 1
            b = order[i]
            slot = ws["free"].pop(0)
            ws["slot_of"][i] = slot
            for (vf, src, key) in block_srcs(b):
                P.op("sp", lambda e, slot=slot, vf=vf, src=src: e.dma_start(out=vf(wring[:, slot, :]), in_=src),
                     reads=[key], writes=["wr%d" % slot], dsem="wr%d" % slot)

    def ws_acquire(b):
        i = ws["nuse"]
        assert order[i] == b, (i, order[i], b)
        assert i < ws["nload"], "weight block not prefetched"
        ws["nuse"] += 1
        slot = ws["slot_of"].pop(i)
        return slot, "wr%d" % slot

    def ws_release(slot):
        ws["free"].append(slot)
        ws_fill()


    def wk8(slot):
        return wring[:, slot, :].rearrange("p (kc n) -> p kc n", kc=8)

    def wk4(slot):
        return wring[:, slot, :].rearrange("p (kc n) -> p kc n", kc=4)

    evac_ctr = [0]

    def evac_copy(out, in_, reads, writes, eng=None):
        if eng is None:
            eng = "act" if evac_ctr[0] % 2 == 0 else "dve"
            evac_ctr[0] += 1
        if eng == "act":
            P.op("act", lambda e: e.copy(out=out, in_=in_), reads=reads, writes=writes)
        else:
            P.op("dve", lambda e: e.tensor_copy(out=out, in_=in_), reads=reads, writes=writes)

    def rms_rstd(src, srckey, col, jslot):
        P.op("act", lambda e: e.activation(out=xs[:, jslot, :], in_=src, func=AF.Square,
                                           accum_out=ssq[:, col:col + 1]),
             reads=[srckey], writes=["ssq%d" % col, "xs%d" % jslot])
        P.op("pool", lambda e: e.tensor_scalar(out=var[:, col:col + 1], in0=ssq[:, col:col + 1], scalar1=1.0 / D,
                                               scalar2=EPS, op0=ALU.mult, op1=ALU.add),
             reads=["ssq%d" % col], writes=["var%d" % col])
        P.op("pool", lambda e: e.tensor_tensor(out=rstd[:, col:col + 1], in0=var[:, col:col + 1], in1=nhalf[:],
                                               op=ALU.pow),
             reads=["var%d" % col, "nhalf"], writes=["rstd%d" % col])

    def norm_prep(src, srckey, col, gb, gkey, xsslot):
        rms_rstd(src, srckey, col, xsslot)
        xsv = xs[:, xsslot, :]
        P.op("dve", lambda e: e.scalar_tensor_tensor(out=xsv, in0=src, scalar=rstd[:, col:col + 1], in1=gb[:],
                                                     op0=ALU.mult, op1=ALU.mult),
             reads=[srckey, "rstd%d" % col, gkey], writes=["xs%d" % xsslot])

    def norm_tr(xsslot, dstT, dstkeys, tcol):
        xsv = xs[:, xsslot, :]
        b = nbank()

        def f(e):
            ins = None
            for c in range(8):
                ins = e.transpose(out=pb16(b, 1024)[:, c * 128:(c + 1) * 128], in_=xsv[:, c * 128:(c + 1) * 128],
                                  identity=ident[:])
            return ins
        P.op("pe", f, reads=["xs%d" % xsslot, "ident"], writes=["ps%d" % b])
        evac_copy(dstT[:, :, tcol:tcol + 128], pb16(b, 1024).rearrange("p (c t) -> p c t", c=8),
                  reads=["ps%d" % b], writes=dstkeys)

    def run_seq(x, y, S):
        J = S // 128
        NU = S // 256
        NS = S // 512

        abuf = {}

        def a_load(j, alt=False):
            sl = j % 2
            if alt:
                buf, key = xr[:, 2 + sl, :], "xr%d" % (2 + sl)
            else:
                buf, key = xa[:, sl, :], "xa%d" % sl
            abuf[j] = (buf, key)
            P.op("sp", lambda e: e.dma_start(out=buf, in_=x[j * 128:(j + 1) * 128, :]),
                 writes=[key], dsem=key)

        def a_norm(j):
            sl = j % 2
            buf, key = abuf.pop(j)
            norm_prep(buf, key, sl, g1b, "g1b", sl)
            if not pending_casts:
                for _ in range(2):
                    if late_casts:
                        cast(*late_casts.pop(0))

        def a_tr(j):
            u, tl = j // 2, j % 2
            us = u % 4
            norm_tr(j % 2, xnT, ["xnT%d" % us], us * 256 + tl * 128)

        def a_back(bid, units):
            slot, wkey = ws_acquire(bid)
            W = wk8(slot)
            for u in units:
                us = u % 4
                xu = xnT[:, :, us * 256:(us + 1) * 256]
                if bid == BK:
                    for cp in range(2):
                        b = nbank()

                        def f(e, cp=cp, b=b, xu=xu):
                            ins = None
                            for ci in range(2):
                                c = cp * 2 + ci
                                for k in range(8):
                                    ins = e.matmul(out=pb(b, ci * 256, ci * 256 + 256),
                                                   lhsT=W[:, k, c * 128:(c + 1) * 128], rhs=xu[:, k, :],
                                                   start=(k == 0), stop=(k == 7))
                            return ins
                        P.op("pe", f, reads=[wkey, "xnT%d" % us], writes=["ps%d" % b])
                        evac_copy(kT[:, cp * 2:cp * 2 + 2, us * 256:(us + 1) * 256],
                                  pb(b).rearrange("p (c t) -> p c t", c=2),
                                  reads=["ps%d" % b], writes=["kT%d" % us])
                else:
                    for tl in range(2):
                        c = 2 * u + tl
                        cs_ = c % 8
                        b = nbank()

                        def f(e, tl=tl, b=b, xu=xu):
                            ins = None
                            for k in range(8):
                                ins = e.matmul(out=pb(b), lhsT=xu[:, k, tl * 128:(tl + 1) * 128], rhs=W[:, k, :],
                                               start=(k == 0), stop=(k == 7))
                            return ins
                        P.op("pe", f, reads=[wkey, "xnT%d" % us], writes=["ps%d" % b])
                        if bid == BV:
                            evac_copy(vtok[:, cs_, :, 0:64], pb(b).rearrange("p (h d) -> p h d", h=8),
                                      reads=["ps%d" % b], writes=["vtok%d" % cs_])
                        else:
                            evac_copy(ptok[:, cs_, :], pb(b), reads=["ps%d" % b], writes=["ptok%d" % cs_])
            ws_release(slot)

        def attention_tile(j, tl, prev_finish):
            if j == 0:
                dl = [(d, 6 - 2 * d) for d in (0, 1, 2, 3)]
            elif j == 1:
                dl = [(d, 6 - 2 * d) for d in (-1, 0, 1, 2)]
            elif j == J - 2:
                dl = [(d, 6 - 2 * d) for d in (-2, -1, 0, 1)]
            elif j == J - 1:
                dl = [(d, 6 - 2 * d) for d in (-3, -2, -1, 0)]
            else:
                dl = [(-2, 16), (-1, 8), (0, 6), (1, 4), (2, 14)]
            combos = [(m, d, idx) for m in range(4) for (d, idx) in dl]
            ob = [nbank(), nbank()]
            reserved.update(ob)
            where = {}
            pv_done = set()
            LAG = 2
            ngroups = (len(combos) + 1) // 2
            last_group = {}
            for ci, (m, d, idx) in enumerate(combos):
                last_group[m] = ci // 2

            def emit_pv(m):
                for hh in range(2):
                    h = 2 * m + hh
                    bo = ob[h // 4]
                    ocol = (h % 4) * 65

                    def fpv(e, h=h, hh=hh, bo=bo, ocol=ocol, wh=dict(where)):
                        ins = None
                        for di, (d, _) in enumerate(dl):
                            ps_, half = wh[(h // 2, d)]
                            c = j + d
                            base = half * 256 + hh * 128
                            ins = e.matmul(out=pb(bo, ocol, ocol + 65), lhsT=pTs[:, ps_, base:base + 128],
                                           rhs=vtok[:, c % 8, h, :], start=(di == 0), stop=(di == len(dl) - 1))
                        return ins
                    ptkeys = sorted(set("pTs%d" % where[(m, d)][0] for (d, _) in dl))
                    vkeys = ["vtok%d" % ((j + d) % 8) for (d, _) in dl]
                    P.op("pe", fpv, reads=ptkeys + vkeys, writes=["ps%d" % bo])

            for g in range(ngroups):
                grp = [(ci, combos[ci]) for ci in range(g * 2, min(g * 2 + 2, len(combos)))]
                b = nbank()
                pslot = pt_ctr[0] % NPT
                pt_ctr[0] += 1
                via_pe = {ci: (ci % 5) != 2 for ci, _ in grp}

                def f(e, grp=grp, b=b, via_pe=via_pe):
                    ins = None
                    for qi, (ci, (m, d, idx)) in enumerate(grp):
                        kcol = ((j + d) % 8) * 128
                        o = pb(b, qi * 256, qi * 256 + 256)
                        if via_pe[ci]:
                            e.matmul(out=o, lhsT=kT[:, m, kcol:kcol + 128], rhs=qbd[:, m, tl, :, :],
                                     start=True, stop=False)
                            ins = e.matmul(out=o, lhsT=ident[:],
                                           rhs=tab[:, 2 * m:2 * m + 2, idx:idx + 2, :].rearrange("p h i q -> p h (i q)"),
                                           start=False, stop=True)
                        else:
                            ins = e.matmul(out=o, lhsT=kT[:, m, kcol:kcol + 128], rhs=qbd[:, m, tl, :, :],
                                           start=True, stop=True)
                    return ins
                kkeys = sorted(set("kT%d" % (((j + d) // 2) % 4) for _, (m, d, idx) in grp))
                qkeys = sorted(set("qbd%d_%d" % (m, hh) for _, (m, d, idx) in grp for hh in range(2)))
                P.op("pe", f, reads=qkeys + kkeys + ["tab", "ident"], writes=["ps%d" % b])
                for qi, (ci, (m, d, idx)) in enumerate(grp):
                    if not via_pe[ci]:
                        pv_ = pb(b, qi * 256, qi * 256 + 256).rearrange("p (h q) -> p h q", h=2)
                        P.op("dve", lambda e, pv_=pv_, m=m, idx=idx: e.tensor_tensor(
                            out=pv_, in0=pv_,
                            in1=tab[:, 2 * m:2 * m + 2, idx:idx + 2, :].rearrange("p h i q -> p h (i q)"), op=ALU.add),
                            reads=["ps%d" % b, "tab"], writes=["ps%d" % b])
                    where[(m, d)] = (pslot, qi)
                n = len(grp) * 256
                P.op("act", lambda e, b=b, pslot=pslot, n=n: e.activation(out=pTs[:, pslot, 0:n], in_=pb(b, 0, n),
                                                                          func=AF.Exp, scale=0.125),
                     reads=["ps%d" % b], writes=["pTs%d" % pslot])
                if g == 5 and prev_finish is not None:
                    prev_finish()
                    prev_finish = None
                for m in range(4):
                    if m not in pv_done and last_group[m] <= g - LAG:
                        pv_done.add(m)
                        emit_pv(m)
            if prev_finish is not None:
                prev_finish()
            for m in range(4):
                if m not in pv_done:
                    pv_done.add(m)
                    emit_pv(m)

            def finish():
                osl = tl % 2
                for hb in range(2):
                    bo = ob[hb]
                    ov = pb(bo, 0, 260).rearrange("p (h d) -> p h d", h=4)
                    P.op("dve", lambda e, ov=ov, hb=hb: e.reciprocal(out=rec[:, osl, hb * 4:(hb + 1) * 4].unsqueeze(2),
                                                                     in_=ov[:, :, 64:65]),
                         reads=["ps%d" % bo], writes=["rec%d_%d" % (osl, hb)])
                    P.op("dve", lambda e, ov=ov, hb=hb: e.tensor_tensor(
                        out=otok[:, osl, hb * 256:(hb + 1) * 256].rearrange("p (h d) -> p h d", h=4),
                        in0=ov[:, :, 0:64],
                        in1=rec[:, osl, hb * 4:(hb + 1) * 4].unsqueeze(2).broadcast_to([128, 4, 64]),
                        op=ALU.mult),
                        reads=["ps%d" % bo, "rec%d_%d" % (osl, hb)], writes=["otok%d_%d" % (osl, hb)])
                reserved.difference_update(ob)
                bt = nbank()

                def ftr(e):
                    ins = None
                    for m in range(4):
                        ins = e.transpose(out=pb16(bt, 512)[:, m * 128:(m + 1) * 128],
                                          in_=otok[:, osl, m * 128:(m + 1) * 128], identity=ident[:])
                    return ins
                P.op("pe", ftr, reads=["otok%d_0" % osl, "otok%d_1" % osl, "ident"], writes=["ps%d" % bt])
                evac_copy(oT[:, :, tl * 128:(tl + 1) * 128], pb16(bt, 512).rearrange("p (c t) -> p c t", c=4),
                          reads=["ps%d" % bt], writes=["oT"] + HID_OT)
            return finish

        def stage_b(s):
            XN = xnT[:, :, (s % 2) * 512:(s % 2) * 512 + 512]
            xnkeys = ["xnT%d" % ((2 * s) % 4), "xnT%d" % ((2 * s + 1) % 4)]
            nxt = [jj for jj in range(4 * s + 6, 4 * s + 10) if jj < J]
            nunits = [u for u in (2 * s + 3, 2 * s + 4) if u < NU]
            P.op("sp", lambda e: e.dma_start(out=xn2T[:].rearrange("p c t -> p (c t)").rearrange("p (kc n) -> p kc n", kc=4),
                                             in_=wb_pp.rearrange("(kc p) n -> p kc n", p=128)),
                 reads=["wb_pp"], writes=["xn2T"], dsem="ppx")
            for jj in nxt[0:2]:
                a_load(jj)
            for tl in range(4):
                j = 4 * s + tl
                P.op("sp", lambda e, j=j, tl=tl: e.dma_start(out=xr[:, tl, :], in_=x[j * 128:(j + 1) * 128, :]),
                     writes=["xr%d" % tl] + (["xr1p"] if tl == 1 else []), dsem="xr%d" % tl)
            slot, wkey = ws_acquire(BQ)
            W = wk8(slot)
            for c in range(4):
                b = nbank()

                def f(e, c=c, b=b, W=W):
                    ins = None
                    for k in range(8):
                        ins = e.matmul(out=pb(b), lhsT=W[:, k, c * 128:(c + 1) * 128], rhs=XN[:, k, :],
                                       start=(k == 0), stop=(k == 7))
                    return ins
                P.op("pe", f, reads=[wkey] + xnkeys, writes=["ps%d" % b])
                for hh in range(2):
                    evac_copy(qbd[hh * 64:(hh + 1) * 64, c, :, hh, :],
                              pb(b)[hh * 64:(hh + 1) * 64, :].rearrange("p (t q) -> p t q", t=4),
                              reads=["ps%d" % b], writes=["qbd%d_%d" % (c, hh)])
            ws_release(slot)
            for tl in range(4):
                j = 4 * s + tl
                b = nbank()

                def fp(e, j=j, b=b):
                    ins = None
                    for g in range(4):
                        srcs = []
                        if j > 0:
                            srcs.append((j - 1, g * 5 + 0))
                        srcs.append((j, g * 5 + (3 if j == 0 else (4 if j == J - 1 else 1))))
                        if j < J - 1:
                            srcs.append((j + 1, g * 5 + 2))
                        for si, (c, v) in enumerate(srcs):
                            ins = e.matmul(out=pb(b, g * 128, (g + 1) * 128),
                                           lhsT=ptok[:, c % 8, g * 128:(g + 1) * 128], rhs=amat[:, v, :],
                                           start=(si == 0), stop=(si == len(srcs) - 1))
                    return ins
                pkeys = ["ptok%d" % (c % 8) for c in (j - 1, j, j + 1) if 0 <= c < J]
                P.op("pe", fp, reads=pkeys + ["amat"], writes=["ps%d" % b])
                evac_copy(pooledT[:, :, tl * 128:(tl + 1) * 128], pb(b).rearrange("p (g t) -> p g t", g=4),
                          reads=["ps%d" % b], writes=["pooledT"] + HID_PO)
            def emit_mixed():
                for g in range(4):
                    b = nbank()
                    P.op("pe", lambda e, g=g, b=b: e.matmul(out=pb(b), lhsT=wgrp[:, g, :], rhs=pooledT[:, g, :],
                                                            start=True, stop=True),
                         reads=["wgrp", "pooledT"], writes=["ps%d" % b])
                    P.op("dve", lambda e, g=g, b=b: e.tensor_scalar(out=mixedT[:, g, :], in0=pb(b), scalar1=psT[:, g:g + 1],
                                                                    scalar2=None, op0=ALU.mult),
                         reads=["ps%d" % b, "psT"], writes=["mixedT"] + HID_MX)
            fin = None
            for tl in range(4):
                fin = attention_tile(4 * s + tl, tl, fin)
                if tl == 0:
                    emit_mixed()
            for jj in nxt[0:2]:
                a_norm(jj)
            for jj in nxt[2:4]:
                a_load(jj)
            fin_pending = [fin]
            gslot, gkey = ws_acquire(BG[0])
            npslot, npkey = ws_acquire(BNP)
            ppkey = "xn2T"
            WPP, WNP = xn2T[:].rearrange("p c t -> p (c t)").rearrange("p (kc n) -> p kc n", kc=4), wk4(npslot)
            for bi in range(4):
                if bi > 0:
                    gslot, gkey = ws_acquire(BG[bi])
                WG = wk8(gslot)
                for i in range(2):
                    m = 2 * bi + i
                    gb_ = m % 2
                    for br in range(2):
                        b = nbank()
                        col0 = br * 256 + i * 128

                        def fg(e, b=b, col0=col0, WG=WG):
                            ins = None
                            for k in range(8):
                                ins = e.matmul(out=pb(b), lhsT=WG[:, k, col0:col0 + 128], rhs=XN[:, k, :],
                                               start=(k == 0), stop=(k == 7))
                            return ins
                        P.op("pe", fg, reads=[gkey] + xnkeys, writes=["ps%d" % b])
                        bcol = br * 8 + m
                        P.op("act", lambda e, b=b, br=br, gb_=gb_, bcol=bcol: e.activation(
                            out=tg[:, br, gb_, :], in_=pb(b), func=AF.Tanh, bias=hbT[:, bcol:bcol + 1], scale=0.5),
                            reads=["ps%d" % b, "hbT"], writes=["tg%d_%d" % (br, gb_)])
                        if br == 1 and fin_pending:
                            fin_pending.pop()()
                        b2 = nbank()
                        Wp = WPP if br == 0 else WNP
                        src = mixedT if br == 0 else oT

                        def fpr(e, b2=b2, Wp=Wp, src=src, m=m):
                            ins = None
                            for k in range(4):
                                ins = e.matmul(out=pb(b2), lhsT=Wp[:, k, m * 128:(m + 1) * 128], rhs=src[:, k, :],
                                               start=(k == 0), stop=(k == 3))
                            return ins
                        P.op("pe", fpr, reads=[ppkey if br == 0 else npkey, "mixedT" if br == 0 else "oT"],
                             writes=["ps%d" % b2])
                        P.op("dve", lambda e, b2=b2, br=br, gb_=gb_: e.scalar_tensor_tensor(
                            out=ug[:, br, :], in0=tg[:, br, gb_, :], scalar=1.0, in1=pb(b2),
                            op0=ALU.add, op1=ALU.mult),
                            reads=["ps%d" % b2, "tg%d_%d" % (br, gb_)], writes=["ug%d" % br])
                    P.op("dve", lambda e, m=m: e.tensor_tensor(out=mT[:, m, :], in0=ug[:, 0, :],
                                                               in1=ug[:, 1, :], op=ALU.add),
                         reads=["ug0", "ug1"], writes=["mT"])
                ws_release(gslot)
                if bi == 1:
                    for jj in nxt[0:2]:
                        a_tr(jj)
                    for jj in nxt[2:4]:
                        a_norm(jj)
            ws_release(npslot)
            for hf in range(2):
                slot, wkey = ws_acquire(BWO[hf])
                W = wk8(slot)
                for tl in range(4):
                    b = nbank()

                    def fo(e, b=b, tl=tl, W=W):
                        ins = None
                        for k in range(8):
                            ins = e.matmul(out=pb(b), lhsT=mT[:, k, tl * 128:(tl + 1) * 128], rhs=W[:, k, :],
                                           start=(k == 0), stop=(k == 7))
                        return ins
                    P.op("pe", fo, reads=[wkey, "mT"], writes=["ps%d" % b])
                    xv = xr[:, tl, hf * 512:(hf + 1) * 512]
                    P.op("dve", lambda e, b=b, xv=xv: e.scalar_tensor_tensor(out=xv, in0=pb(b), scalar=0.5, in1=xv,
                                                                             op0=ALU.mult, op1=ALU.add),
                         reads=["ps%d" % b, "xr%d" % tl], writes=["xr%d" % tl])
                ws_release(slot)
                if hf == 0:
                    for jj in nxt[2:4]:
                        a_tr(jj)
            def n_prep(tl):
                norm_prep(xr[:, tl, :], "xr%d" % tl, 2 + tl, g2b, "g2b", tl % 2)

            def n_tr(tl):
                norm_tr(tl % 2, xn2T, ["xn2T"], tl * 128)
            n_prep(0)
            n_prep(1)
            if nunits:
                a_back(BK, nunits)
            n_tr(0)
            n_tr(1)
            n_prep(2)
            n_prep(3)
            if nunits:
                a_back(BV, nunits)
            n_tr(2)
            n_tr(3)
            if nunits:
                a_back(BP, nunits)
            for half in range(2):
                for fb in range(4):
                    slot, wkey = ws_acquire(12 + half * 8 + fb)
                    W = wk8(slot)
                    for c4 in range(4):
                        b = nbank()
                        hc = fb * 4 + c4

                        def f1(e, b=b, c4=c4, W=W):
                            ins = None
                            for k in range(8):
                                ins = e.matmul(out=pb(b), lhsT=W[:, k, c4 * 128:(c4 + 1) * 128], rhs=xn2T[:, k, :],
                                               start=(k == 0), stop=(k == 7))
                            return ins
                        P.op("pe", f1, reads=[wkey, "xn2T"], writes=["ps%d" % b])
                        rs_ = 0
                        P.op("act", lambda e, b=b, rs_=rs_: e.activation(out=rl[:, rs_, :], in_=pb(b), func=AF.Relu),
                             reads=["ps%d" % b], writes=["rl%d" % rs_])
                        P.op("dve", lambda e, rs_=rs_, hc=hc: e.tensor_tensor(out=hidT[:, hc, :], in0=rl[:, rs_, :],
                                                                              in1=rl[:, rs_, :], op=ALU.mult),
                             reads=["rl%d" % rs_],
                             writes=["hid%d" % hc] + (["oT"] if hc < 4 else ["pooledT"] if hc < 8 else ["mixedT"] if hc < 12 else []))
                    ws_release(slot)
                for ch in range(2):
                    banks = [nbank() for _ in range(4)]
                    for kg in range(2):
                        slot, wkey = ws_acquire(16 + half * 8 + ch * 2 + kg)
                        W = wk8(slot)

                        def f2(e, kg=kg, W=W, banks=banks):
                            ins = None
                            for k in range(8):
                                for tl in range(4):
                                    ins = e.matmul(out=pb(banks[tl]), lhsT=hidT[:, kg * 8 + k, tl * 128:(tl + 1) * 128],
                                                   rhs=W[:, k, :], start=(kg == 0 and k == 0),
                                                   stop=(kg == 1 and k == 7))
                            return ins
                        P.op("pe", f2, reads=[wkey] + ["hid%d" % (kg * 8 + k) for k in range(8)],
                             writes=["ps%d" % b for b in banks])
                        ws_release(slot)
                    for tl in range(4):
                        xv = xr[:, tl, ch * 512:(ch + 1) * 512]
                        P.op("dve", lambda e, b=banks[tl], xv=xv: e.tensor_tensor(out=xv, in0=pb(b), in1=xv, op=ALU.add),
                             reads=["ps%d" % banks[tl], "xr%d" % tl], writes=["xr%d" % tl])
            for tl in range(4):
                j = 4 * s + tl
                rms_rstd(xr[:, tl, :], "xr%d" % tl, 2 + tl, tl % 2)
                P.op("dve", lambda e, tl=tl: e.scalar_tensor_tensor(out=xr[:, tl, :], in0=xr[:, tl, :],
                                                                    scalar=rstd[:, 2 + tl:3 + tl], in1=nfb[:],
                                                                    op0=ALU.mult, op1=ALU.mult),
                     reads=["xr%d" % tl, "rstd%d" % (2 + tl), "nfb"], writes=["xr%d" % tl])
                P.op("sp", lambda e, j=j, tl=tl: e.dma_start(out=y[j * 128:(j + 1) * 128, :], in_=xr[:, tl, :]),
                     reads=["xr%d" % tl], writes=["yout%d" % tl], dsem="yo%d" % tl)

        def pro_extras(jj):
            for c_ in pending_casts.pop(jj, []):
                cast(*c_)
            if table_steps and jj >= 2:
                table_steps.pop(0)()
        a_load(0)
        a_load(1)
        a_load(2, alt=True)
        a_load(3, alt=True)
        if table_steps:
            traw()
        ws_fill()
        a_norm(0)
        pro_extras(0)
        a_norm(1)
        pro_extras(1)
        a_tr(0)
        a_norm(2)
        pro_extras(2)
        a_load(4)
        a_tr(1)
        a_norm(3)
        pro_extras(3)
        a_load(5)
        for bid in (BK, BV, BP):
            a_back(bid, [0])
        a_tr(2)
        a_norm(4)
        pro_extras(4)
        a_tr(3)
        a_norm(5)
        pro_extras(5)
        for bid in (BK, BV, BP):
            a_back(bid, [1])
        a_tr(4)
        a_tr(5)
        for bid in (BK, BV, BP):
            a_back(bid, [2])
        while table_steps:
            table_steps.pop(0)()
        for s in range(NS):
            stage_b(s)

    for i, S in enumerate(seq_lens):
        run_seq(xin[i], yout[i], S)
    assert ws["nuse"] == len(order), (ws["nuse"], len(order))
    P.op("sp", lambda e: None, reads=["yout%d" % t for t in range(4)])

    with nc.Block() as block:
        @block.tensor
        def _(e):
            P.replay("pe", e)

        @block.scalar
        def _(e):
            P.replay("act", e)

        @block.vector
        def _(e):
            P.replay("dve", e)

        @block.gpsimd
        def _(e):
            P.replay("pool", e)

        @block.sync
        def _(e):
            P.replay("sp", e)
    P.stack.close()
    return nc


_CONSTS = None


def _consts():
    global _CONSTS
    if _CONSTS is None:
        m01, mneg = _col_masks()
        _CONSTS = dict(c_ident=np.eye(128, dtype=np.float32), c_amat=_pool_mats(), c_m01=m01, c_mneg=mneg)
    return _CONSTS


def make_in_maps(xs_list, weights):
    c = _consts()
    maps = []
    for xs_ in xs_list:
        m = dict(c)
        for i, x in enumerate(xs_):
            m["x%d" % i] = np.ascontiguousarray(x, dtype=np.float32)
        m.update(weights)
        maps.append(m)
    return maps


def prep_weights(norm_mix, w_in, b_gate, w_pool_grp, pool_scale, w_pool_proj, rpb, w_na_proj, w_out, norm_mlp,
                 w_ff1, w_ff2, norm_final):
    f = lambda a: np.ascontiguousarray(np.asarray(a, dtype=np.float32))
    return dict(norm_mix=f(norm_mix[0]), w_in=f(w_in[0]), b_gate=f(b_gate[0]), w_pool_grp=f(w_pool_grp[0]),
                pool_scale=f(pool_scale[0]), w_pool_proj=f(w_pool_proj[0]), rpb=f(rpb[0]), w_na_proj=f(w_na_proj[0]),
                w_out=f(w_out[0]), norm_mlp=f(norm_mlp[0]), w_ff1=f(w_ff1[0]), w_ff2=f(w_ff2[0]),
                norm_final=f(norm_final))


def kernel(x_prompt, x_sample, norm_mix, w_in, b_gate, w_pool_grp, pool_scale, w_pool_proj, rpb, w_na_proj, w_out,
           norm_mlp, w_ff1, w_ff2, norm_final):
    x_prompt = np.asarray(x_prompt)
    x_sample = np.asarray(x_sample)
    ncores = 8
    SP_, SS_ = x_prompt.shape[1], x_sample.shape[1]
    nc = build_nc((SP_, SS_))
    weights = prep_weights(norm_mix, w_in, b_gate, w_pool_grp, pool_scale, w_pool_proj, rpb, w_na_proj, w_out,
                           norm_mlp, w_ff1, w_ff2, norm_final)
    in_maps = make_in_maps([[x_prompt[i], x_sample[i]] for i in range(ncores)], weights)
    res = run_bass_kernel_spmd(nc, in_maps, core_ids=list(range(ncores)))
    yp = np.stack([np.asarray(r["y0"], dtype=np.float32) for r in res.results], axis=0)
    ys = np.stack([np.asarray(r["y1"], dtype=np.float32) for r in res.results], axis=0)
    return (yp, ys)
```

```python
import contextlib
import numpy as np
import concourse.bass as bass
import concourse.mybir as mybir
from concourse.bass_utils import run_bass_kernel_spmd

F32 = mybir.dt.float32
BF16 = mybir.dt.bfloat16
ALU = mybir.AluOpType
AF = mybir.ActivationFunctionType

D = 1024
NB = 5
NEG = -240000.0
EPS = 1e-6
POOL_W = (2, 4, 8, 16)

BK, BV, BP, BQ = 0, 1, 2, 3
BG = (4, 5, 6, 7)
BPP, BNP = 8, 9
BWO = (10, 11)
NBLK = 28


class Prog:
    def __init__(self, nc):
        self.nc = nc
        self.stack = contextlib.ExitStack()
        self.eng = {}
        self.nsem = 0
        for name in ("pe", "act", "dve", "pool", "sp"):
            self.eng[name] = dict(ops=[], sem=self.new_sem("e_" + name), tick=0, known={}, snap=None)
        self.lastw = {}
        self.readers = {}
        self.dsems = {}

    def new_sem(self, name):
        self.nsem += 1
        return self.stack.enter_context(self.nc.semaphore(name))

    def sb(self, name, shape, dtype):
        return self.stack.enter_context(self.nc.sbuf_tensor(name, shape, dtype))

    def op(self, eng, fn, reads=(), writes=(), dsem=None):
        E = self.eng[eng]
        known = E["known"]
        waits = {}

        def need(tok):
            if tok is None:
                return
            s, v, clk = tok
            sid = id(s)
            if eng == "pe" and s is E["sem"]:
                return
            if known.get(sid, 0) >= v:
                return
            if waits.get(sid, (None, 0, None))[1] < v:
                waits[sid] = tok

        for k in reads:
            need(self.lastw.get(k))
        for k in writes:
            need(self.lastw.get(k))
            for tok in self.readers.get(k, {}).values():
                need(tok)
        if waits:
            for sid, (s, v, clk) in waits.items():
                if known.get(sid, 0) < v:
                    known[sid] = v
                if clk is not None:
                    for k2, v2 in clk.items():
                        if known.get(k2, 0) < v2:
                            known[k2] = v2
            E["snap"] = None
        if E["snap"] is None:
            E["snap"] = dict(known)
        if dsem is None:
            E["tick"] += 1
            tok = (E["sem"], E["tick"], E["snap"])
            amount = 1
        else:
            if dsem not in self.dsems:
                self.dsems[dsem] = [self.new_sem("d_" + dsem), 0]
            ds = self.dsems[dsem]
            ds[1] += 16
            tok = (ds[0], ds[1], None)
            amount = 16
        E["ops"].append(([(s, v) for (s, v, c) in waits.values()], fn, tok[0], amount))
        sid = id(tok[0])
        for k in reads:
            self.readers.setdefault(k, {})[sid] = tok
        for k in writes:
            self.lastw[k] = tok
            self.readers[k] = {}
        return tok

    def replay(self, name, e):
        for waits, fn, sem, amount in self.eng[name]["ops"]:
            for (s, v) in waits:
                e.wait_ge(s, v)
            ins = fn(e)
            if ins is not None:
                ins.then_inc(sem, amount)


def _pool_mats():
    A = np.zeros((128, 20, 128), np.float32)
    for g, w in enumerate(POOL_W):
        h = w // 2
        for tl in range(128):
            for off in range(-h, h):
                src = tl + off
                val = 1.0 / w
                if src < 0:
                    A[128 + src, g * 5 + 0, tl] += val
                elif src >= 128:
                    A[src - 128, g * 5 + 2, tl] += val
                else:
                    A[src, g * 5 + 1, tl] += val
            A[tl, g * 5 + 1, tl] -= 1.0
            lo, hi = max(tl - h, 0), tl + h
            for src in range(lo, min(hi, 128)):
                A[src, g * 5 + 3, tl] += 1.0 / (hi - lo)
            A[tl, g * 5 + 3, tl] -= 1.0
            lo, hi = tl - h, min(tl + h, 128)
            for src in range(max(lo, 0), hi):
                A[src, g * 5 + 4, tl] += 1.0 / (hi - lo)
            A[tl, g * 5 + 4, tl] -= 1.0
    return A


def _col_masks():
    m = np.zeros((128, 64), np.float32)
    for a in range(2):
        for kc in range(64):
            for qc in range(64):
                cs = min(max(qc - 8, 0), 48)
                if cs <= kc < cs + 16:
                    m[a * 64 + kc, qc] = 1.0
    return m, (m - 1.0) * (-NEG)


def build_nc(seq_lens):
    nc = bass.Bass("TRN2", target_bir_lowering=False)
    P = Prog(nc)
    dram = {}

    def din(name, shape):
        dram[name] = nc.dram_tensor(name, list(shape), F32, kind="ExternalInput").ap()
        return dram[name]

    xin, yout = [], []
    for i, S in enumerate(seq_lens):
        xin.append(din("x%d" % i, (S, D)))
    for i, S in enumerate(seq_lens):
        yout.append(nc.dram_tensor("y%d" % i, [S, D], F32, kind="ExternalOutput").ap())
    norm_mix = din("norm_mix", (D,))
    w_in = din("w_in", (D, 4096))
    b_gate = din("b_gate", (2048,))
    w_grp = din("w_pool_grp", (4, 128, 128))
    pool_scale = din("pool_scale", (512,))
    w_pp = din("w_pool_proj", (512, D))
    rpb = din("rpb", (8, 15, 31))
    w_np = din("w_na_proj", (512, D))
    w_out = din("w_out", (D, D))
    norm_mlp = din("norm_mlp", (D,))
    w_ff1 = din("w_ff1", (D, 4096))
    w_ff2 = din("w_ff2", (4096, D))
    norm_final = din("norm_final", (D,))
    c_ident = din("c_ident", (128, 128))
    c_amat = din("c_amat", (128, 20, 128))
    c_m01 = din("c_m01", (128, 64))
    c_mneg = din("c_mneg", (128, 64))

    wb_in = nc.dram_tensor("wb_in", [D, 4096], BF16, kind="Internal").ap()
    wb_pp = nc.dram_tensor("wb_pp", [512, D], BF16, kind="Internal").ap()
    wb_np = nc.dram_tensor("wb_np", [512, D], BF16, kind="Internal").ap()
    wb_out = nc.dram_tensor("wb_out", [D, D], BF16, kind="Internal").ap()
    wb_f1 = nc.dram_tensor("wb_f1", [D, 4096], BF16, kind="Internal").ap()
    wb_f2 = nc.dram_tensor("wb_f2", [4096, D], BF16, kind="Internal").ap()
    rpbp = nc.dram_tensor("rpbp", [8, 15, 160], F32, kind="Internal").ap()
    bexp = nc.dram_tensor("bexp", [120, 64, 64], F32, kind="Internal").ap()

    ident = P.sb("ident", [128, 128], BF16)
    amat = P.sb("amat", [128, 20, 128], BF16)
    wgrp = P.sb("wgrp", [128, 4, 128], BF16)
    tab = P.sb("tab", [128, 8, 18, 64], BF16)
    m01 = P.sb("m01", [128, 64], F32)
    mneg = P.sb("mneg", [128, 64], F32)
    qbd = P.sb("qbd", [128, 4, 4, 2, 128], BF16)
    g1b = P.sb("g1b", [128, D], F32)
    g2b = P.sb("g2b", [128, D], F32)
    nfb = P.sb("nfb", [128, D], F32)
    bgT = P.sb("bgT", [128, 16], F32)
    hbT = P.sb("hbT", [128, 16], F32)
    psT = P.sb("psT", [128, 4], F32)
    nhalf = P.sb("nhalf", [128, 1], F32)
    ssq = P.sb("ssq", [128, 8], F32)
    var = P.sb("var", [128, 8], F32)
    rstd = P.sb("rstd", [128, 8], F32)
    rec = P.sb("rec", [128, 2, 8], F32)
    xa = P.sb("xa", [128, 2, D], F32)
    xs = P.sb("xs", [128, 2, D], BF16)
    xnT = P.sb("xnT", [128, 8, 1024], BF16)
    kT = P.sb("kT", [128, 4, 1024], BF16)
    vtok = P.sb("vtok", [128, 8, 8, 65], BF16)
    ptok = P.sb("ptok", [128, 8, 512], BF16)
    arena = P.sb("arena", [128, 12288], BF16)
    oT = arena[:, 0:2048].rearrange("p (c t) -> p c t", c=4)
    pooledT = arena[:, 2048:4096].rearrange("p (c t) -> p c t", c=4)
    mixedT = arena[:, 4096:6144].rearrange("p (c t) -> p c t", c=4)
    hidT = arena[:, 0:8192].rearrange("p (c t) -> p c t", c=16)
    mT = arena[:, 8192:12288].rearrange("p (c t) -> p c t", c=8)
    HID_OT = ["hid%d" % c for c in range(0, 4)]
    HID_PO = ["hid%d" % c for c in range(4, 8)]
    HID_MX = ["hid%d" % c for c in range(8, 12)]
    xn2T = P.sb("xn2T", [128, 8, 512], BF16)
    tg = P.sb("tg", [128, 2, 2, 512], F32)
    ug = P.sb("ug", [128, 2, 512], F32)
    rl = P.sb("rl", [128, 1, 512], F32)
    NPT = 5
    pt_ctr = [0]
    pTs = P.sb("pTs", [128, NPT, 512], BF16)
    otok = P.sb("otok", [128, 2, 512], BF16)
    xr = P.sb("xr", [128, 4, D], F32)
    wring = P.sb("wring", [128, NB, 4096], BF16)
    psum = P.stack.enter_context(nc.psum_tensor("psum", [128, 4096], F32))

    bank_ctr = [0]
    reserved = set()

    def nbank():
        while True:
            b = bank_ctr[0] % 8
            bank_ctr[0] += 1
            if b not in reserved:
                return b

    def pb(b, lo=0, hi=512):
        return psum[:, b * 512 + lo: b * 512 + hi]

    def pb16(b, n):
        return psum[:, b * 512: b * 512 + n // 2].bitcast(BF16)

    def cast(dst, src, key):
        P.op("pool", lambda e: e.dma_start(out=dst, in_=src, max_dma_last_dim=8192), writes=[key], dsem=key)

    cast(ident[:], c_ident, "ident")
    cast(amat[:], c_amat, "amat")
    cast(wgrp[:], w_grp.rearrange("g c d -> c g d"), "wgrp")
    cast(wb_in, w_in, "wb_in")
    pending_casts = {1: [(wb_pp, w_pp, "wb_pp"), (wb_np, w_np, "wb_np"), (wb_out, w_out, "wb_out")]}
    for pi in range(8):
        pending_casts.setdefault(2 + pi // 2, []).append((wb_f1[pi * 128:(pi + 1) * 128, :], w_ff1[pi * 128:(pi + 1) * 128, :], "wb_f1"))
    late_casts = []
    for pi in range(8):
        late_casts.append((wb_f2[pi * 512:(pi + 1) * 512, :], w_ff2[pi * 512:(pi + 1) * 512, :], "wb_f2_%d" % (pi // 2)))

    def kview(w, r0, c0, ncol=512):
        return w[r0:r0 + 1024, c0:c0 + ncol].rearrange("(kc p) n -> p kc n", p=128)

    def block_srcs(b):
        v8 = lambda t: t.rearrange("p (kc n) -> p kc n", kc=8)
        if b == BK:
            return [(v8, kview(wb_in, 0, 1024), "wb_in")]
        if b == BV:
            return [(v8, kview(wb_in, 0, 1536), "wb_in")]
        if b == BP:
            return [(v8, kview(wb_in, 0, 0), "wb_in")]
        if b == BQ:
            return [(v8, kview(wb_in, 0, 512), "wb_in")]
        if b in BG:
            bi = BG.index(b)
            out = []
            for br in range(2):
                c0 = 2048 + br * 1024 + 2 * bi * 128
                out.append((lambda t, br=br: t.rearrange("p (kc br n) -> p kc br n", kc=8, br=2)[:, :, br, :],
                            kview(wb_in, 0, c0, 256), "wb_in"))
            return out
        if b == BNP:
            return [(lambda t: t.rearrange("p (kc n) -> p kc n", kc=4), wb_np.rearrange("(kc p) n -> p kc n", p=128), "wb_np")]
        if b in BWO:
            return [(v8, kview(wb_out, 0, BWO.index(b) * 512), "wb_out")]
        half, r = (b - 12) // 8, (b - 12) % 8
        if r < 4:
            return [(v8, kview(wb_f1, 0, (half * 4 + r) * 512), "wb_f1")]
        ch, kg = (r - 4) // 2, (r - 4) % 2
        return [(v8, kview(wb_f2, (half * 2 + kg) * 1024, ch * 512), "wb_f2_%d" % (half * 2 + kg))]

    def load(dst, src, key, slow=False):
        P.op("sp", lambda e: e.dma_start(out=dst, in_=src, allow_slow_non_contiguous=slow), writes=[key], dsem=key)

    load(m01[:], c_m01, "m01")
    load(mneg[:], c_mneg, "mneg")
    load(g1b[:], norm_mix.partition_broadcast(128), "g1b")
    load(g2b[:], norm_mlp.partition_broadcast(128), "g2b")
    load(nfb[:], norm_final.partition_broadcast(128), "nfb")
    load(bgT[:], b_gate.rearrange("(c p) -> p c", p=128), "bgT", slow=True)
    load(psT[:], pool_scale.rearrange("(c p) -> p c", p=128), "psT", slow=True)
    P.op("dve", lambda e: e.memset(nhalf[:], -0.5), writes=["nhalf"])
    P.op("dve", lambda e: e.memset(qbd[:].rearrange("p a b c d -> p (a b c d)"), 0.0), writes=["qbd%d_%d" % (c, hh) for c in range(4) for hh in range(2)])
    P.op("dve", lambda e: e.tensor_scalar(out=hbT[:], in0=bgT[:], scalar1=0.5, scalar2=None, op0=ALU.mult),
         reads=["bgT"], writes=["hbT"])
    P.op("dve", lambda e: e.memset(vtok[:, :, :, 64:65], 1.0), writes=["vtok%d" % c for c in range(8)])

    xrf = xr[:].rearrange("p a b -> p (a b)")
    table_steps = []

    def traw():
        P.op("sp", lambda e: e.dma_start(out=xr[0:120, 1, 256:287], in_=rpb.rearrange("h a c -> (h a) c")),
             writes=["xr1"], dsem="rawrpb")

    def tstep0():
        pad = xr[0:120, 1, 0:160]
        P.op("dve", lambda e: e.memset(pad, 0.0), writes=["xr1p"])

        def frev(e):
            ins = None
            for i in range(31):
                ins = e.tensor_copy(out=xr[0:120, 1, 64 + i:65 + i], in_=xr[0:120, 1, 256 + 30 - i:257 + 30 - i])
            return ins
        P.op("dve", frev, reads=["xr1", "xr1p"], writes=["xr1p"])
        P.op("sp", lambda e: e.dma_start(out=rpbp.rearrange("h a c -> (h a) c"), in_=pad),
             reads=["xr1p"], writes=["rpbp"], dsem="rpbp")

    def tstep1():
        for h in range(8):
            src = bass.AP(tensor=rpbp.tensor, offset=h * 15 * 160 + 64 + 15, ap=[[160, 15], [-1, 64], [1, 64]])
            P.op("sp", lambda e, h=h, src=src: e.dma_start(out=bexp[h * 15:(h + 1) * 15], in_=src),
                 reads=["rpbp"], writes=["bexp"], dsem="bexp")

    stage = xrf[:, 0:3584].rearrange("p (h i q) -> p h i q", h=4, i=14)
    stf = xrf[:, 0:3584].rearrange("p (r q) -> p r q", q=64)

    STK = ["st%d_%d" % (hl, a) for hl in range(4) for a in range(2)]

    def tstage(hh):
        for hl in range(4):
            h = hh * 4 + hl
            for a in range(2):
                src = bass.AP(tensor=bexp.tensor, offset=(h * 15 + a + 13) * 4096,
                              ap=[[64, 64], [-4096, 14], [1, 64]])
                P.op("sp", lambda e, a=a, hl=hl, src=src: e.dma_start(out=stage[a * 64:(a + 1) * 64, hl], in_=src),
                     reads=["bexp"], writes=["st%d_%d" % (hl, a)], dsem="stage")

    def tdve(hh):
        P.op("dve", lambda e: e.scalar_tensor_tensor(out=stf, in0=stf, scalar=8.0,
                                                     in1=m01[:].unsqueeze(1).broadcast_to([128, 56, 64]),
                                                     op0=ALU.mult, op1=ALU.mult),
             reads=STK + ["m01"], writes=STK)
        P.op("dve", lambda e: e.tensor_tensor(
            out=tab[:, hh * 4:(hh + 1) * 4, 0:14, :], in0=stage,
            in1=mneg[:].unsqueeze(1).unsqueeze(1).broadcast_to([128, 4, 14, 64]), op=ALU.add),
            reads=STK + ["mneg"], writes=["tab"])

    def tfinal():
        P.op("dve", lambda e: e.memset(tab[:, :, 14, :], NEG), writes=["tab"])
        P.op("dve", lambda e: e.tensor_copy(out=tab[:, :, 15, :], in_=tab[:, :, 3, :]), reads=["tab"], writes=["tab"])
        P.op("dve", lambda e: e.memset(tab[64:128, :, 15, :], NEG), writes=["tab"])
        P.op("dve", lambda e: e.tensor_copy(out=tab[:, :, 16:18, :], in_=tab[:, :, 10:12, :]), reads=["tab"], writes=["tab"])
        P.op("dve", lambda e: e.memset(tab[0:64, :, 17, :], NEG), writes=["tab"])

    table_steps += [tstep0, tstep1, lambda: tstage(0), lambda: (tdve(0), tstage(1)), lambda: (tdve(1), tfinal())]

    order = []
    for S in seq_lens:
        order += [BK, BV, BP, BK, BV, BP, BK, BV, BP]
        for s in range(S // 512):
            order += [BQ, BG[0], BNP, BG[1], BG[2], BG[3], BWO[0], BWO[1]]
            if 2 * s + 3 < S // 256:
                order += [BK, BV, BP]
            for half in range(2):
                order += [12 + half * 8 + i for i in range(4)]
                order += [16 + half * 8 + i for i in range(4)]
    ws = dict(nload=0, nuse=0, free=list(range(NB)), slot_of={})

    def ws_fill():
        while ws["free"] and ws["nload"] < len(order):
            i = ws["nload"]
            ws["nload"] += 1
            b = order[i]
            slot = ws["free"].pop(0)
            ws["slot_of"][i] = slot
            for (vf, src, key) in block_srcs(b):
                P.op("sp", lambda e, slot=slot, vf=vf, src=src: e.dma_start(out=vf(wring[:, slot, :]), in_=src),
                     reads=[key], writes=["wr%d" % slot], dsem="wr%d" % slot)

    def ws_acquire(b):
        i = ws["nuse"]
        assert order[i] == b, (i, order[i], b)
        assert i < ws["nload"], "weight block not prefetched"
        ws["nuse"] += 1
        slot = ws["slot_of"].pop(i)
        return slot, "wr%d" % slot

    def ws_release(slot):
        ws["free"].append(slot)
        ws_fill()


    def wk8(slot):
        return wring[:, slot, :].rearrange("p (kc n) -> p kc n", kc=8)

    def wk4(slot):
        return wring[:, slot, :].rearrange("p (kc n) -> p kc n", kc=4)

    evac_ctr = [0]

    def evac_copy(out, in_, reads, writes, eng=None):
        if eng is None:
            eng = "act" if evac_ctr[0] % 2 == 0 else "dve"
            evac_ctr[0] += 1
        if eng == "act":
            P.op("act", lambda e: e.copy(out=out, in_=in_), reads=reads, writes=writes)
        else:
            P.op("dve", lambda e: e.tensor_copy(out=out, in_=in_), reads=reads, writes=writes)

    def rms_rstd(src, srckey, col, jslot):
        P.op("act", lambda e: e.activation(out=xs[:, jslot, :], in_=src, func=AF.Square,
                                           accum_out=ssq[:, col:col + 1]),
             reads=[srckey], writes=["ssq%d" % col, "xs%d" % jslot])
        P.op("pool", lambda e: e.tensor_scalar(out=var[:, col:col + 1], in0=ssq[:, col:col + 1], scalar1=1.0 / D,
                                               scalar2=EPS, op0=ALU.mult, op1=ALU.add),
             reads=["ssq%d" % col], writes=["var%d" % col])
        P.op("pool", lambda e: e.tensor_tensor(out=rstd[:, col:col + 1], in0=var[:, col:col + 1], in1=nhalf[:],
                                               op=ALU.pow),
             reads=["var%d" % col, "nhalf"], writes=["rstd%d" % col])

    def norm_prep(src, srckey, col, gb, gkey, xsslot):
        rms_rstd(src, srckey, col, xsslot)
        xsv = xs[:, xsslot, :]
        P.op("dve", lambda e: e.scalar_tensor_tensor(out=xsv, in0=src, scalar=rstd[:, col:col + 1], in1=gb[:],
                                                     op0=ALU.mult, op1=ALU.mult),
             reads=[srckey, "rstd%d" % col, gkey], writes=["xs%d" % xsslot])

    def norm_tr(xsslot, dstT, dstkeys, tcol):
        xsv = xs[:, xsslot, :]
        b = nbank()

        def f(e):
            ins = None
            for c in range(8):
                ins = e.transpose(out=pb16(b, 1024)[:, c * 128:(c + 1) * 128], in_=xsv[:, c * 128:(c + 1) * 128],
                                  identity=ident[:])
            return ins
        P.op("pe", f, reads=["xs%d" % xsslot, "ident"], writes=["ps%d" % b])
        evac_copy(dstT[:, :, tcol:tcol + 128], pb16(b, 1024).rearrange("p (c t) -> p c t", c=8),
                  reads=["ps%d" % b], writes=dstkeys)

    def run_seq(x, y, S):
        J = S // 128
        NU = S // 256
        NS = S // 512

        abuf = {}

        def a_load(j, alt=False):
            sl = j % 2
            if alt:
                buf, key = xr[:, 2 + sl, :], "xr%d" % (2 + sl)
            else:
                buf, key = xa[:, sl, :], "xa%d" % sl
            abuf[j] = (buf, key)
            P.op("sp", lambda e: e.dma_start(out=buf, in_=x[j * 128:(j + 1) * 128, :]),
                 writes=[key], dsem=key)

        def a_norm(j):
            sl = j % 2
            buf, key = abuf.pop(j)
            norm_prep(buf, key, sl, g1b, "g1b", sl)
            if not pending_casts:
                for _ in range(2):
                    if late_casts:
                        cast(*late_casts.pop(0))

        def a_tr(j):
            u, tl = j // 2, j % 2
            us = u % 4
            norm_tr(j % 2, xnT, ["xnT%d" % us], us * 256 + tl * 128)

        def a_back(bid, units):
            slot, wkey = ws_acquire(bid)
            W = wk8(slot)
            for u in units:
                us = u % 4
                xu = xnT[:, :, us * 256:(us + 1) * 256]
                if bid == BK:
                    for cp in range(2):
                        b = nbank()

                        def f(e, cp=cp, b=b, xu=xu):
                            ins = None
                            for ci in range(2):
                                c = cp * 2 + ci
                                for k in range(8):
                                    ins = e.matmul(out=pb(b, ci * 256, ci * 256 + 256),
                                                   lhsT=W[:, k, c * 128:(c + 1) * 128], rhs=xu[:, k, :],
                                                   start=(k == 0), stop=(k == 7))
                            return ins
                        P.op("pe", f, reads=[wkey, "xnT%d" % us], writes=["ps%d" % b])
                        evac_copy(kT[:, cp * 2:cp * 2 + 2, us * 256:(us + 1) * 256],
                                  pb(b).rearrange("p (c t) -> p c t", c=2),
                                  reads=["ps%d" % b], writes=["kT%d" % us])
                else:
                    for tl in range(2):
                        c = 2 * u + tl
                        cs_ = c % 8
                        b = nbank()

                        def f(e, tl=tl, b=b, xu=xu):
                            ins = None
                            for k in range(8):
                                ins = e.matmul(out=pb(b), lhsT=xu[:, k, tl * 128:(tl + 1) * 128], rhs=W[:, k, :],
                                               start=(k == 0), stop=(k == 7))
                            return ins
                        P.op("pe", f, reads=[wkey, "xnT%d" % us], writes=["ps%d" % b])
                        if bid == BV:
                            evac_copy(vtok[:, cs_, :, 0:64], pb(b).rearrange("p (h d) -> p h d", h=8),
                                      reads=["ps%d" % b], writes=["vtok%d" % cs_])
                        else:
                            evac_copy(ptok[:, cs_, :], pb(b), reads=["ps%d" % b], writes=["ptok%d" % cs_])
            ws_release(slot)

        def attention_tile(j, tl, prev_finish):
            if j == 0:
                dl = [(d, 6 - 2 * d) for d in (0, 1, 2, 3)]
            elif j == 1:
                dl = [(d, 6 - 2 * d) for d in (-1, 0, 1, 2)]
            elif j == J - 2:
                dl = [(d, 6 - 2 * d) for d in (-2, -1, 0, 1)]
            elif j == J - 1:
                dl = [(d, 6 - 2 * d) for d in (-3, -2, -1, 0)]
            else:
                dl = [(-2, 16), (-1, 8), (0, 6), (1, 4), (2, 14)]
            combos = [(m, d, idx) for m in range(4) for (d, idx) in dl]
            ob = [nbank(), nbank()]
            reserved.update(ob)
            where = {}
            pv_done = set()
            LAG = 2
            ngroups = (len(combos) + 1) // 2
            last_group = {}
            for ci, (m, d, idx) in enumerate(combos):
                last_group[m] = ci // 2

            def emit_pv(m):
                for hh in range(2):
                    h = 2 * m + hh
                    bo = ob[h // 4]
                    ocol = (h % 4) * 65

                    def fpv(e, h=h, hh=hh, bo=bo, ocol=ocol, wh=dict(where)):
                        ins = None
                        for di, (d, _) in enumerate(dl):
                            ps_, half = wh[(h // 2, d)]
                            c = j + d
                            base = half * 256 + hh * 128
                            ins = e.matmul(out=pb(bo, ocol, ocol + 65), lhsT=pTs[:, ps_, base:base + 128],
                                           rhs=vtok[:, c % 8, h, :], start=(di == 0), stop=(di == len(dl) - 1))
                        return ins
                    ptkeys = sorted(set("pTs%d" % where[(m, d)][0] for (d, _) in dl))
                    vkeys = ["vtok%d" % ((j + d) % 8) for (d, _) in dl]
                    P.op("pe", fpv, reads=ptkeys + vkeys, writes=["ps%d" % bo])

            for g in range(ngroups):
                grp = [(ci, combos[ci]) for ci in range(g * 2, min(g * 2 + 2, len(combos)))]
                b = nbank()
                pslot = pt_ctr[0] % NPT
                pt_ctr[0] += 1
                via_pe = {ci: (ci % 5) != 2 for ci, _ in grp}

                def f(e, grp=grp, b=b, via_pe=via_pe):
                    ins = None
                    for qi, (ci, (m, d, idx)) in enumerate(grp):
                        kcol = ((j + d) % 8) * 128
                        o = pb(b, qi * 256, qi * 256 + 256)
                        if via_pe[ci]:
                            e.matmul(out=o, lhsT=kT[:, m, kcol:kcol + 128], rhs=qbd[:, m, tl, :, :],
                                     start=True, stop=False)
                            ins = e.matmul(out=o, lhsT=ident[:],
                                           rhs=tab[:, 2 * m:2 * m + 2, idx:idx + 2, :].rearrange("p h i q -> p h (i q)"),
                                           start=False, stop=True)
                        else:
                            ins = e.matmul(out=o, lhsT=kT[:, m, kcol:kcol + 128], rhs=qbd[:, m, tl, :, :],
                                           start=True, stop=True)
                    return ins
                kkeys = sorted(set("kT%d" % (((j + d) // 2) % 4) for _, (m, d, idx) in grp))
                qkeys = sorted(set("qbd%d_%d" % (m, hh) for _, (m, d, idx) in grp for hh in range(2)))
                P.op("pe", f, reads=qkeys + kkeys + ["tab", "ident"], writes=["ps%d" % b])
                for qi, (ci, (m, d, idx)) in enumerate(grp):
                    if not via_pe[ci]:
                        pv_ = pb(b, qi * 256, qi * 256 + 256).rearrange("p (h q) -> p h q", h=2)
                        P.op("dve", lambda e, pv_=pv_, m=m, idx=idx: e.tensor_tensor(
                            out=pv_, in0=pv_,
                            in1=tab[:, 2 * m:2 * m + 2, idx:idx + 2, :].rearrange("p h i q -> p h (i q)"), op=ALU.add),
                            reads=["ps%d" % b, "tab"], writes=["ps%d" % b])
                    where[(m, d)] = (pslot, qi)
                n = len(grp) * 256
                P.op("act", lambda e, b=b, pslot=pslot, n=n: e.activation(out=pTs[:, pslot, 0:n], in_=pb(b, 0, n),
                                                                          func=AF.Exp, scale=0.125),
                     reads=["ps%d" % b], writes=["pTs%d" % pslot])
                if g == 5 and prev_finish is not None:
                    prev_finish()
                    prev_finish = None
                for m in range(4):
                    if m not in pv_done and last_group[m] <= g - LAG:
                        pv_done.add(m)
                        emit_pv(m)
            if prev_finish is not None:
                prev_finish()
            for m in range(4):
                if m not in pv_done:
                    pv_done.add(m)
                    emit_pv(m)

            def finish():
                osl = tl % 2
                for hb in range(2):
                    bo = ob[hb]
                    ov = pb(bo, 0, 260).rearrange("p (h d) -> p h d", h=4)
                    P.op("dve", lambda e, ov=ov, hb=hb: e.reciprocal(out=rec[:, osl, hb * 4:(hb + 1) * 4].unsqueeze(2),
                                                                     in_=ov[:, :, 64:65]),
                         reads=["ps%d" % bo], writes=["rec%d_%d" % (osl, hb)])
                    P.op("dve", lambda e, ov=ov, hb=hb: e.tensor_tensor(
                        out=otok[:, osl, hb * 256:(hb + 1) * 256].rearrange("p (h d) -> p h d", h=4),
                        in0=ov[:, :, 0:64],
                        in1=rec[:, osl, hb * 4:(hb + 1) * 4].unsqueeze(2).broadcast_to([128, 4, 64]),
                        op=ALU.mult),
                        reads=["ps%d" % bo, "rec%d_%d" % (osl, hb)], writes=["otok%d_%d" % (osl, hb)])
                reserved.difference_update(ob)
                bt = nbank()

                def ftr(e):
                    ins = None
                    for m in range(4):
                        ins = e.transpose(out=pb16(bt, 512)[:, m * 128:(m + 1) * 128],
                                          in_=otok[:, osl, m * 128:(m + 1) * 128], identity=ident[:])
                    return ins
                P.op("pe", ftr, reads=["otok%d_0" % osl, "otok%d_1" % osl, "ident"], writes=["ps%d" % bt])
                evac_copy(oT[:, :, tl * 128:(tl + 1) * 128], pb16(bt, 512).rearrange("p (c t) -> p c t", c=4),
                          reads=["ps%d" % bt], writes=["oT"] + HID_OT)
            return finish

        def stage_b(s):
            XN = xnT[:, :, (s % 2) * 512:(s % 2) * 512 + 512]
            xnkeys = ["xnT%d" % ((2 * s) % 4), "xnT%d" % ((2 * s + 1) % 4)]
            nxt = [jj for jj in range(4 * s + 6, 4 * s + 10) if jj < J]
            nunits = [u for u in (2 * s + 3, 2 * s + 4) if u < NU]
            P.op("sp", lambda e: e.dma_start(out=xn2T[:].rearrange("p c t -> p (c t)").rearrange("p (kc n) -> p kc n", kc=4),
                                             in_=wb_pp.rearrange("(kc p) n -> p kc n", p=128)),
                 reads=["wb_pp"], writes=["xn2T"], dsem="ppx")
            for jj in nxt[0:2]:
                a_load(jj)
            for tl in range(4):
                j = 4 * s + tl
                P.op("sp", lambda e, j=j, tl=tl: e.dma_start(out=xr[:, tl, :], in_=x[j * 128:(j + 1) * 128, :]),
                     writes=["xr%d" % tl] + (["xr1p", "xr1"] if tl == 1 else STK if tl == 0 else []), dsem="xr%d" % tl)
            slot, wkey = ws_acquire(BQ)
            W = wk8(slot)
            for c in range(4):
                b = nbank()

                def f(e, c=c, b=b, W=W):
                    ins = None
                    for k in range(8):
                        ins = e.matmul(out=pb(b), lhsT=W[:, k, c * 128:(c + 1) * 128], rhs=XN[:, k, :],
                                       start=(k == 0), stop=(k == 7))
                    return ins
                P.op("pe", f, reads=[wkey] + xnkeys, writes=["ps%d" % b])
                for hh in range(2):
                    evac_copy(qbd[hh * 64:(hh + 1) * 64, c, :, hh, :],
                              pb(b)[hh * 64:(hh + 1) * 64, :].rearrange("p (t q) -> p t q", t=4),
                              reads=["ps%d" % b], writes=["qbd%d_%d" % (c, hh)])
            ws_release(slot)
            for tl in range(4):
                j = 4 * s + tl
                b = nbank()

                def fp(e, j=j, b=b):
                    ins = None
                    for g in range(4):
                        srcs = []
                        if j > 0:
                            srcs.append((j - 1, g * 5 + 0))
                        srcs.append((j, g * 5 + (3 if j == 0 else (4 if j == J - 1 else 1))))
                        if j < J - 1:
                            srcs.append((j + 1, g * 5 + 2))
                        for si, (c, v) in enumerate(srcs):
                            ins = e.matmul(out=pb(b, g * 128, (g + 1) * 128),
                                           lhsT=ptok[:, c % 8, g * 128:(g + 1) * 128], rhs=amat[:, v, :],
                                           start=(si == 0), stop=(si == len(srcs) - 1))
                    return ins
                pkeys = ["ptok%d" % (c % 8) for c in (j - 1, j, j + 1) if 0 <= c < J]
                P.op("pe", fp, reads=pkeys + ["amat"], writes=["ps%d" % b])
                evac_copy(pooledT[:, :, tl * 128:(tl + 1) * 128], pb(b).rearrange("p (g t) -> p g t", g=4),
                          reads=["ps%d" % b], writes=["pooledT"] + HID_PO)
            def emit_mixed():
                for g in range(4):
                    b = nbank()
                    P.op("pe", lambda e, g=g, b=b: e.matmul(out=pb(b), lhsT=wgrp[:, g, :], rhs=pooledT[:, g, :],
                                                            start=True, stop=True),
                         reads=["wgrp", "pooledT"], writes=["ps%d" % b])
                    P.op("dve", lambda e, g=g, b=b: e.tensor_scalar(out=mixedT[:, g, :], in0=pb(b), scalar1=psT[:, g:g + 1],
                                                                    scalar2=None, op0=ALU.mult),
                         reads=["ps%d" % b, "psT"], writes=["mixedT"] + HID_MX)
            fin = None
            for tl in range(4):
                fin = attention_tile(4 * s + tl, tl, fin)
                if tl == 0:
                    emit_mixed()
            for jj in nxt[0:2]:
                a_norm(jj)
            for jj in nxt[2:4]:
                a_load(jj)
            fin_pending = [fin]
            gslot, gkey = ws_acquire(BG[0])
            npslot, npkey = ws_acquire(BNP)
            ppkey = "xn2T"
            WPP, WNP = xn2T[:].rearrange("p c t -> p (c t)").rearrange("p (kc n) -> p kc n", kc=4), wk4(npslot)
            for bi in range(4):
                if bi > 0:
                    gslot, gkey = ws_acquire(BG[bi])
                WG = wk8(gslot)
                for i in range(2):
                    m = 2 * bi + i
                    gb_ = m % 2
                    for br in range(2):
                        b = nbank()
                        col0 = br * 256 + i * 128

                        def fg(e, b=b, col0=col0, WG=WG):
                            ins = None
                            for k in range(8):
                                ins = e.matmul(out=pb(b), lhsT=WG[:, k, col0:col0 + 128], rhs=XN[:, k, :],
                                               start=(k == 0), stop=(k == 7))
                            return ins
                        P.op("pe", fg, reads=[gkey] + xnkeys, writes=["ps%d" % b])
                        bcol = br * 8 + m
                        P.op("act", lambda e, b=b, br=br, gb_=gb_, bcol=bcol: e.activation(
                            out=tg[:, br, gb_, :], in_=pb(b), func=AF.Tanh, bias=hbT[:, bcol:bcol + 1], scale=0.5),
                            reads=["ps%d" % b, "hbT"], writes=["tg%d_%d" % (br, gb_)])
                        if br == 1 and fin_pending:
                            fin_pending.pop()()
                        b2 = nbank()
                        Wp = WPP if br == 0 else WNP
                        src = mixedT if br == 0 else oT

                        def fpr(e, b2=b2, Wp=Wp, src=src, m=m):
                            ins = None
                            for k in range(4):
                                ins = e.matmul(out=pb(b2), lhsT=Wp[:, k, m * 128:(m + 1) * 128], rhs=src[:, k, :],
                                               start=(k == 0), stop=(k == 3))
                            return ins
                        P.op("pe", fpr, reads=[ppkey if br == 0 else npkey, "mixedT" if br == 0 else "oT"],
                             writes=["ps%d" % b2])
                        P.op("dve", lambda e, b2=b2, br=br, gb_=gb_: e.scalar_tensor_tensor(
                            out=ug[:, br, :], in0=tg[:, br, gb_, :], scalar=1.0, in1=pb(b2),
                            op0=ALU.add, op1=ALU.mult),
                            reads=["ps%d" % b2, "tg%d_%d" % (br, gb_)], writes=["ug%d" % br])
                    P.op("dve", lambda e, m=m: e.tensor_tensor(out=mT[:, m, :], in0=ug[:, 0, :],
                                                               in1=ug[:, 1, :], op=ALU.add),
                         reads=["ug0", "ug1"], writes=["mT"])
                ws_release(gslot)
                if bi == 1:
                    for jj in nxt[0:2]:
                        a_tr(jj)
                    for jj in nxt[2:4]:
                        a_norm(jj)
            ws_release(npslot)
            for hf in range(2):
                slot, wkey = ws_acquire(BWO[hf])
                W = wk8(slot)
                for tl in range(4):
                    b = nbank()

                    def fo(e, b=b, tl=tl, W=W):
                        ins = None
                        for k in range(8):
                            ins = e.matmul(out=pb(b), lhsT=mT[:, k, tl * 128:(tl + 1) * 128], rhs=W[:, k, :],
                                           start=(k == 0), stop=(k == 7))
                        return ins
                    P.op("pe", fo, reads=[wkey, "mT"], writes=["ps%d" % b])
                    xv = xr[:, tl, hf * 512:(hf + 1) * 512]
                    P.op("dve", lambda e, b=b, xv=xv: e.scalar_tensor_tensor(out=xv, in0=pb(b), scalar=0.5, in1=xv,
                                                                             op0=ALU.mult, op1=ALU.add),
                         reads=["ps%d" % b, "xr%d" % tl], writes=["xr%d" % tl])
                ws_release(slot)
                if hf == 0:
                    for jj in nxt[2:4]:
                        a_tr(jj)
            def n_prep(tl):
                norm_prep(xr[:, tl, :], "xr%d" % tl, 2 + tl, g2b, "g2b", tl % 2)

            def n_tr(tl):
                norm_tr(tl % 2, xn2T, ["xn2T"], tl * 128)
            n_prep(0)
            n_prep(1)
            if nunits:
                a_back(BK, nunits)
            n_tr(0)
            n_tr(1)
            n_prep(2)
            n_prep(3)
            if nunits:
                a_back(BV, nunits)
            n_tr(2)
            n_tr(3)
            if nunits:
                a_back(BP, nunits)
            for half in range(2):
                for fb in range(4):
                    slot, wkey = ws_acquire(12 + half * 8 + fb)
                    W = wk8(slot)
                    for c4 in range(4):
                        b = nbank()
                        hc = fb * 4 + c4

                        def f1(e, b=b, c4=c4, W=W):
                            ins = None
                            for k in range(8):
                                ins = e.matmul(out=pb(b), lhsT=W[:, k, c4 * 128:(c4 + 1) * 128], rhs=xn2T[:, k, :],
                                               start=(k == 0), stop=(k == 7))
                            return ins
                        P.op("pe", f1, reads=[wkey, "xn2T"], writes=["ps%d" % b])
                        rs_ = 0
                        P.op("act", lambda e, b=b, rs_=rs_: e.activation(out=rl[:, rs_, :], in_=pb(b), func=AF.Relu),
                             reads=["ps%d" % b], writes=["rl%d" % rs_])
                        P.op("dve", lambda e, rs_=rs_, hc=hc: e.tensor_tensor(out=hidT[:, hc, :], in0=rl[:, rs_, :],
                                                                              in1=rl[:, rs_, :], op=ALU.mult),
                             reads=["rl%d" % rs_],
                             writes=["hid%d" % hc] + (["oT"] if hc < 4 else ["pooledT"] if hc < 8 else ["mixedT"] if hc < 12 else []))
                    ws_release(slot)
                for ch in range(2):
                    banks = [nbank() for _ in range(4)]
                    for kg in range(2):
                        slot, wkey = ws_acquire(16 + half * 8 + ch * 2 + kg)
                        W = wk8(slot)

                        def f2(e, kg=kg, W=W, banks=banks):
                            ins = None
                            for k in range(8):
                                for tl in range(4):
                                    ins = e.matmul(out=pb(banks[tl]), lhsT=hidT[:, kg * 8 + k, tl * 128:(tl + 1) * 128],
                                                   rhs=W[:, k, :], start=(kg == 0 and k == 0),
                                                   stop=(kg == 1 and k == 7))
                            return ins
                        P.op("pe", f2, reads=[wkey] + ["hid%d" % (kg * 8 + k) for k in range(8)],
                             writes=["ps%d" % b for b in banks])
                        ws_release(slot)
                    for tl in range(4):
                        xv = xr[:, tl, ch * 512:(ch + 1) * 512]
                        P.op("dve", lambda e, b=banks[tl], xv=xv: e.tensor_tensor(out=xv, in0=pb(b), in1=xv, op=ALU.add),
                             reads=["ps%d" % banks[tl], "xr%d" % tl], writes=["xr%d" % tl])
            for tl in range(4):
                j = 4 * s + tl
                rms_rstd(xr[:, tl, :], "xr%d" % tl, 2 + tl, tl % 2)
                P.op("dve", lambda e, tl=tl: e.scalar_tensor_tensor(out=xr[:, tl, :], in0=xr[:, tl, :],
                                                                    scalar=rstd[:, 2 + tl:3 + tl], in1=nfb[:],
                                                                    op0=ALU.mult, op1=ALU.mult),
                     reads=["xr%d" % tl, "rstd%d" % (2 + tl), "nfb"], writes=["xr%d" % tl])
                P.op("sp", lambda e, j=j, tl=tl: e.dma_start(out=y[j * 128:(j + 1) * 128, :], in_=xr[:, tl, :]),
                     reads=["xr%d" % tl], writes=["yout%d" % tl], dsem="yo%d" % tl)

        def pro_extras(jj):
            for c_ in pending_casts.pop(jj, []):
                cast(*c_)
            if table_steps and jj >= 2:
                table_steps.pop(0)()
        a_load(0)
        a_load(1)
        a_load(2, alt=True)
        a_load(3, alt=True)
        if table_steps:
            traw()
        ws_fill()
        a_norm(0)
        pro_extras(0)
        a_norm(1)
        pro_extras(1)
        a_tr(0)
        a_norm(2)
        pro_extras(2)
        a_load(4)
        a_tr(1)
        a_norm(3)
        pro_extras(3)
        a_load(5)
        for bid in (BK, BV, BP):
            a_back(bid, [0])
        a_tr(2)
        a_norm(4)
        pro_extras(4)
        a_tr(3)
        a_norm(5)
        pro_extras(5)
        for bid in (BK, BV, BP):
            a_back(bid, [1])
        a_tr(4)
        a_tr(5)
        for bid in (BK, BV, BP):
            a_back(bid, [2])
        while table_steps:
            table_steps.pop(0)()
        for s in range(NS):
            stage_b(s)

    for i, S in enumerate(seq_lens):
        run_seq(xin[i], yout[i], S)
    assert ws["nuse"] == len(order), (ws["nuse"], len(order))
    P.op("sp", lambda e: None, reads=["yout%d" % t for t in range(4)])

    with nc.Block() as block:
        @block.tensor
        def _(e):
            P.replay("pe", e)

        @block.scalar
        def _(e):
            P.replay("act", e)

        @block.vector
        def _(e):
            P.replay("dve", e)

        @block.gpsimd
        def _(e):
            P.replay("pool", e)

        @block.sync
        def _(e):
            P.replay("sp", e)
    P.stack.close()
    return nc


_CONSTS = None


def _consts():
    global _CONSTS
    if _CONSTS is None:
        m01, mneg = _col_masks()
        _CONSTS = dict(c_ident=np.eye(128, dtype=np.float32), c_amat=_pool_mats(), c_m01=m01, c_mneg=mneg)
    return _CONSTS


def make_in_maps(xs_list, weights):
    c = _consts()
    maps = []
    for xs_ in xs_list:
        m = dict(c)
        for i, x in enumerate(xs_):
            m["x%d" % i] = np.ascontiguousarray(x, dtype=np.float32)
        m.update(weights)
        maps.append(m)
    return maps


def prep_weights(norm_mix, w_in, b_gate, w_pool_grp, pool_scale, w_pool_proj, rpb, w_na_proj, w_out, norm_mlp,
                 w_ff1, w_ff2, norm_final):
    f = lambda a: np.ascontiguousarray(np.asarray(a, dtype=np.float32))
    return dict(norm_mix=f(norm_mix[0]), w_in=f(w_in[0]), b_gate=f(b_gate[0]), w_pool_grp=f(w_pool_grp[0]),
                pool_scale=f(pool_scale[0]), w_pool_proj=f(w_pool_proj[0]), rpb=f(rpb[0]), w_na_proj=f(w_na_proj[0]),
                w_out=f(w_out[0]), norm_mlp=f(norm_mlp[0]), w_ff1=f(w_ff1[0]), w_ff2=f(w_ff2[0]),
                norm_final=f(norm_final))


def kernel(x_prompt, x_sample, norm_mix, w_in, b_gate, w_pool_grp, pool_scale, w_pool_proj, rpb, w_na_proj, w_out,
           norm_mlp, w_ff1, w_ff2, norm_final):
    x_prompt = np.asarray(x_prompt)
    x_sample = np.asarray(x_sample)
    ncores = 8
    SP_, SS_ = x_prompt.shape[1], x_sample.shape[1]
    nc = build_nc((SP_, SS_))
    weights = prep_weights(norm_mix, w_in, b_gate, w_pool_grp, pool_scale, w_pool_proj, rpb, w_na_proj, w_out,
                           norm_mlp, w_ff1, w_ff2, norm_final)
    in_maps = make_in_maps([[x_prompt[i], x_sample[i]] for i in range(ncores)], weights)
    res = run_bass_kernel_spmd(nc, in_maps, core_ids=list(range(ncores)))
    yp = np.stack([np.asarray(r["y0"], dtype=np.float32) for r in res.results], axis=0)
    ys = np.stack([np.asarray(r["y1"], dtype=np.float32) for r in res.results], axis=0)
    return (yp, ys)
```

```python
import contextlib
import numpy as np
import concourse.bass as bass
import concourse.mybir as mybir
from concourse.bass_utils import run_bass_kernel_spmd

F32 = mybir.dt.float32
BF16 = mybir.dt.bfloat16
ALU = mybir.AluOpType
AF = mybir.ActivationFunctionType

D = 1024
NB = 5
NEG = -240000.0
EPS = 1e-6
POOL_W = (2, 4, 8, 16)

BK, BV, BP, BQ = 0, 1, 2, 3
BG = (4, 5, 6, 7)
BPP, BNP = 8, 9
BWO = (10, 11)
NBLK = 28


class Prog:
    def __init__(self, nc):
        self.nc = nc
        self.stack = contextlib.ExitStack()
        self.eng = {}
        self.nsem = 0
        for name in ("pe", "act", "dve", "pool", "sp"):
            self.eng[name] = dict(ops=[], sem=self.new_sem("e_" + name), tick=0, known={}, snap=None)
        self.lastw = {}
        self.readers = {}
        self.dsems = {}

    def new_sem(self, name):
        self.nsem += 1
        return self.stack.enter_context(self.nc.semaphore(name))

    def sb(self, name, shape, dtype):
        return self.stack.enter_context(self.nc.sbuf_tensor(name, shape, dtype))

    def op(self, eng, fn, reads=(), writes=(), dsem=None):
        E = self.eng[eng]
        known = E["known"]
        waits = {}

        def need(tok):
            if tok is None:
                return
            s, v, clk = tok
            sid = id(s)
            if eng == "pe" and s is E["sem"]:
                return
            if known.get(sid, 0) >= v:
                return
            if waits.get(sid, (None, 0, None))[1] < v:
                waits[sid] = tok

        for k in reads:
            need(self.lastw.get(k))
        for k in writes:
            need(self.lastw.get(k))
            for tok in self.readers.get(k, {}).values():
                need(tok)
        if waits:
            for sid, (s, v, clk) in waits.items():
                if known.get(sid, 0) < v:
                    known[sid] = v
                if clk is not None:
                    for k2, v2 in clk.items():
                        if known.get(k2, 0) < v2:
                            known[k2] = v2
            E["snap"] = None
        if E["snap"] is None:
            E["snap"] = dict(known)
        if dsem is None:
            E["tick"] += 1
            tok = (E["sem"], E["tick"], E["snap"])
            amount = 1
        else:
            if dsem not in self.dsems:
                self.dsems[dsem] = [self.new_sem("d_" + dsem), 0]
            ds = self.dsems[dsem]
            ds[1] += 16
            tok = (ds[0], ds[1], None)
            amount = 16
        E["ops"].append(([(s, v) for (s, v, c) in waits.values()], fn, tok[0], amount))
        sid = id(tok[0])
        for k in reads:
            self.readers.setdefault(k, {})[sid] = tok
        for k in writes:
            self.lastw[k] = tok
            self.readers[k] = {}
        return tok

    def fence(self, eng, keys):
        E = self.eng[eng]
        known = E["known"]
        waits = {}
        for k in keys:
            toks = [self.lastw.get(k)] + list(self.readers.get(k, {}).values())
            for tok in toks:
                if tok is None:
                    continue
                s_, v, clk = tok
                sid = id(s_)
                if known.get(sid, 0) >= v:
                    continue
                if waits.get(sid, (None, 0, None))[1] < v:
                    waits[sid] = tok
        for sid, (s_, v, clk) in waits.items():
            known[sid] = max(known.get(sid, 0), v)
            if clk is not None:
                for k2, v2 in clk.items():
                    if known.get(k2, 0) < v2:
                        known[k2] = v2
        if waits:
            E["snap"] = None
        E["ops"].append(([(s_, v) for (s_, v, c) in waits.values()], (lambda e: None), None, 0))

    def replay(self, name, e):
        for waits, fn, sem, amount in self.eng[name]["ops"]:
            for (s, v) in waits:
                e.wait_ge(s, v)
            ins = fn(e)
            if ins is not None:
                ins.then_inc(sem, amount)


def _pool_mats():
    A = np.zeros((128, 20, 128), np.float32)
    for g, w in enumerate(POOL_W):
        h = w // 2
        for tl in range(128):
            for off in range(-h, h):
                src = tl + off
                val = 1.0 / w
                if src < 0:
                    A[128 + src, g * 5 + 0, tl] += val
                elif src >= 128:
                    A[src - 128, g * 5 + 2, tl] += val
                else:
                    A[src, g * 5 + 1, tl] += val
            A[tl, g * 5 + 1, tl] -= 1.0
            lo, hi = max(tl - h, 0), tl + h
            for src in range(lo, min(hi, 128)):
                A[src, g * 5 + 3, tl] += 1.0 / (hi - lo)
            A[tl, g * 5 + 3, tl] -= 1.0
            lo, hi = tl - h, min(tl + h, 128)
            for src in range(max(lo, 0), hi):
                A[src, g * 5 + 4, tl] += 1.0 / (hi - lo)
            A[tl, g * 5 + 4, tl] -= 1.0
    return A


def _col_masks():
    m = np.zeros((128, 64), np.float32)
    for a in range(2):
        for kc in range(64):
            for qc in range(64):
                cs = min(max(qc - 8, 0), 48)
                if cs <= kc < cs + 16:
                    m[a * 64 + kc, qc] = 1.0
    return m, (m - 1.0) * (-NEG)


def build_nc(seq_lens):
    nc = bass.Bass("TRN2", target_bir_lowering=False)
    P = Prog(nc)
    dram = {}

    def din(name, shape):
        dram[name] = nc.dram_tensor(name, list(shape), F32, kind="ExternalInput").ap()
        return dram[name]

    xin, yout = [], []
    for i, S in enumerate(seq_lens):
        xin.append(din("x%d" % i, (S, D)))
    for i, S in enumerate(seq_lens):
        yout.append(nc.dram_tensor("y%d" % i, [S, D], F32, kind="ExternalOutput").ap())
    norm_mix = din("norm_mix", (D,))
    w_in = din("w_in", (D, 4096))
    b_gate = din("b_gate", (2048,))
    w_grp = din("w_pool_grp", (4, 128, 128))
    pool_scale = din("pool_scale", (512,))
    w_pp = din("w_pool_proj", (512, D))
    rpb = din("rpb", (8, 15, 31))
    w_np = din("w_na_proj", (512, D))
    w_out = din("w_out", (D, D))
    norm_mlp = din("norm_mlp", (D,))
    w_ff1 = din("w_ff1", (D, 4096))
    w_ff2 = din("w_ff2", (4096, D))
    norm_final = din("norm_final", (D,))
    c_ident = din("c_ident", (128, 128))
    c_amat = din("c_amat", (128, 20, 128))
    c_m01 = din("c_m01", (128, 64))
    c_mneg = din("c_mneg", (128, 64))

    wb_in = nc.dram_tensor("wb_in", [D, 4096], BF16, kind="Internal").ap()
    wb_pp = nc.dram_tensor("wb_pp", [512, D], BF16, kind="Internal").ap()
    wb_np = nc.dram_tensor("wb_np", [512, D], BF16, kind="Internal").ap()
    wb_out = nc.dram_tensor("wb_out", [D, D], BF16, kind="Internal").ap()
    wb_f1 = nc.dram_tensor("wb_f1", [D, 4096], BF16, kind="Internal").ap()
    wb_f2 = nc.dram_tensor("wb_f2", [4096, D], BF16, kind="Internal").ap()
    rpbp = nc.dram_tensor("rpbp", [8, 15, 160], F32, kind="Internal").ap()
    bexp = nc.dram_tensor("bexp", [120, 64, 64], F32, kind="Internal").ap()

    ident = P.sb("ident", [128, 128], BF16)
    amat = P.sb("amat", [128, 20, 128], BF16)
    wgrp = P.sb("wgrp", [128, 4, 128], BF16)
    tab = P.sb("tab", [128, 8, 18, 64], BF16)
    m01 = P.sb("m01", [128, 64], F32)
    mneg = P.sb("mneg", [128, 64], F32)
    qbd = P.sb("qbd", [128, 4, 4, 2, 128], BF16)
    g1b = P.sb("g1b", [128, D], F32)
    g2b = P.sb("g2b", [128, D], F32)
    nfb = P.sb("nfb", [128, D], F32)
    bgT = P.sb("bgT", [128, 16], F32)
    hbT = P.sb("hbT", [128, 16], F32)
    psT = P.sb("psT", [128, 4], F32)
    nhalf = P.sb("nhalf", [128, 1], F32)
    ssq = P.sb("ssq", [128, 8], F32)
    var = P.sb("var", [128, 8], F32)
    rstd = P.sb("rstd", [128, 8], F32)
    rec = P.sb("rec", [128, 2, 8], F32)
    xa = P.sb("xa", [128, 2, D], F32)
    xs = P.sb("xs", [128, 2, D], BF16)
    xnT = P.sb("xnT", [128, 8, 1024], BF16)
    kT = P.sb("kT", [128, 4, 1024], BF16)
    vtok = P.sb("vtok", [128, 8, 8, 65], BF16)
    ptok = P.sb("ptok", [128, 8, 512], BF16)
    arena = P.sb("arena", [128, 12288], BF16)
    oT = arena[:, 0:2048].rearrange("p (c t) -> p c t", c=4)
    pooledT = arena[:, 2048:4096].rearrange("p (c t) -> p c t", c=4)
    mixedT = arena[:, 4096:6144].rearrange("p (c t) -> p c t", c=4)
    hidT = arena[:, 0:8192].rearrange("p (c t) -> p c t", c=16)
    mT = arena[:, 8192:12288].rearrange("p (c t) -> p c t", c=8)
    HID_OT = ["hid%d" % c for c in range(0, 4)]
    HID_PO = ["hid%d" % c for c in range(4, 8)]
    HID_MX = ["hid%d" % c for c in range(8, 12)]
    xn2T = P.sb("xn2T", [128, 8, 512], BF16)
    tg = P.sb("tg", [128, 2, 2, 512], F32)
    ug = P.sb("ug", [128, 2, 512], F32)
    rl = P.sb("rl", [128, 1, 512], F32)
    NPT = 5
    pt_ctr = [0]
    pTs = P.sb("pTs", [128, NPT, 512], BF16)
    otok = P.sb("otok", [128, 2, 512], BF16)
    xr = P.sb("xr", [128, 4, D], F32)
    wring = P.sb("wring", [128, NB, 4096], BF16)
    psum = P.stack.enter_context(nc.psum_tensor("psum", [128, 4096], F32))

    bank_ctr = [0]
    reserved = set()

    def nbank():
        while True:
            b = bank_ctr[0] % 8
            bank_ctr[0] += 1
            if b not in reserved:
                return b

    def pb(b, lo=0, hi=512):
        return psum[:, b * 512 + lo: b * 512 + hi]

    def pb16(b, n):
        return psum[:, b * 512: b * 512 + n // 2].bitcast(BF16)

    def cast(dst, src, key):
        P.op("pool", lambda e: e.dma_start(out=dst, in_=src, max_dma_last_dim=8192), writes=[key], dsem=key)

    cast(ident[:], c_ident, "ident")
    cast(amat[:], c_amat, "amat")
    cast(wgrp[:], w_grp.rearrange("g c d -> c g d"), "wgrp")
    cast(wb_in, w_in, "wb_in")
    pending_casts = {1: [(wb_pp, w_pp, "wb_pp"), (wb_np, w_np, "wb_np"), (wb_out, w_out, "wb_out")]}
    for pi in range(8):
        pending_casts.setdefault(2 + pi // 2, []).append((wb_f1[pi * 128:(pi + 1) * 128, :], w_ff1[pi * 128:(pi + 1) * 128, :], "wb_f1"))
    late_casts = []
    for pi in range(8):
        late_casts.append((wb_f2[pi * 512:(pi + 1) * 512, :], w_ff2[pi * 512:(pi + 1) * 512, :], "wb_f2_%d" % (pi // 2)))

    def kview(w, r0, c0, ncol=512):
        return w[r0:r0 + 1024, c0:c0 + ncol].rearrange("(kc p) n -> p kc n", p=128)

    def block_srcs(b):
        v8 = lambda t: t.rearrange("p (kc n) -> p kc n", kc=8)
        if b == BK:
            return [(v8, kview(wb_in, 0, 1024), "wb_in")]
        if b == BV:
            return [(v8, kview(wb_in, 0, 1536), "wb_in")]
        if b == BP:
            return [(v8, kview(wb_in, 0, 0), "wb_in")]
        if b == BQ:
            return [(v8, kview(wb_in, 0, 512), "wb_in")]
        if b in BG:
            bi = BG.index(b)
            out = []
            for br in range(2):
                c0 = 2048 + br * 1024 + 2 * bi * 128
                out.append((lambda t, br=br: t.rearrange("p (kc br n) -> p kc br n", kc=8, br=2)[:, :, br, :],
                            kview(wb_in, 0, c0, 256), "wb_in"))
            return out
        if b == BNP:
            return [(lambda t: t.rearrange("p (kc n) -> p kc n", kc=4), wb_np.rearrange("(kc p) n -> p kc n", p=128), "wb_np")]
        if b in BWO:
            return [(v8, kview(wb_out, 0, BWO.index(b) * 512), "wb_out")]
        half, r = (b - 12) // 8, (b - 12) % 8
        if r < 4:
            return [(v8, kview(wb_f1, 0, (half * 4 + r) * 512), "wb_f1")]
        ch, kg = (r - 4) // 2, (r - 4) % 2
        return [(v8, kview(wb_f2, (half * 2 + kg) * 1024, ch * 512), "wb_f2_%d" % (half * 2 + kg))]

    def load(dst, src, key, slow=False):
        P.op("sp", lambda e: e.dma_start(out=dst, in_=src, allow_slow_non_contiguous=slow), writes=[key], dsem=key)

    load(m01[:], c_m01, "m01")
    load(mneg[:], c_mneg, "mneg")
    load(g1b[:], norm_mix.partition_broadcast(128), "g1b")
    load(g2b[:], norm_mlp.partition_broadcast(128), "g2b")
    load(nfb[:], norm_final.partition_broadcast(128), "nfb")
    load(bgT[:], b_gate.rearrange("(c p) -> p c", p=128), "bgT", slow=True)
    load(psT[:], pool_scale.rearrange("(c p) -> p c", p=128), "psT", slow=True)
    P.op("dve", lambda e: e.memset(nhalf[:], -0.5), writes=["nhalf"])
    P.op("dve", lambda e: e.memset(qbd[:].rearrange("p a b c d -> p (a b c d)"), 0.0), writes=["qbd%d_%d" % (c, hh) for c in range(4) for hh in range(2)])
    P.op("dve", lambda e: e.tensor_scalar(out=hbT[:], in0=bgT[:], scalar1=0.5, scalar2=None, op0=ALU.mult),
         reads=["bgT"], writes=["hbT"])
    P.op("dve", lambda e: e.memset(vtok[:, :, :, 64:65], 1.0), writes=["vtok%d" % c for c in range(8)])

    xrf = xr[:].rearrange("p a b -> p (a b)")
    table_steps = []

    def traw():
        P.op("sp", lambda e: e.dma_start(out=xr[0:120, 1, 256:287], in_=rpb.rearrange("h a c -> (h a) c")),
             writes=["xr1"], dsem="rawrpb")

    def tstep0():
        pad = xr[0:120, 1, 0:160]
        P.op("dve", lambda e: e.memset(pad, 0.0), writes=["xr1"])

        def frev(e):
            ins = None
            for i in range(31):
                ins = e.tensor_copy(out=xr[0:120, 1, 64 + i:65 + i], in_=xr[0:120, 1, 256 + 30 - i:257 + 30 - i])
            return ins
        P.op("dve", frev, reads=["xr1"], writes=["xr1"])
        P.op("sp", lambda e: e.dma_start(out=rpbp.rearrange("h a c -> (h a) c"), in_=pad),
             reads=["xr1"], writes=["rpbp"], dsem="rpbp")

    def tstep1():
        for h in range(8):
            src = bass.AP(tensor=rpbp.tensor, offset=h * 15 * 160 + 64 + 15, ap=[[160, 15], [-1, 64], [1, 64]])
            P.op("sp", lambda e, h=h, src=src: e.dma_start(out=bexp[h * 15:(h + 1) * 15], in_=src),
                 reads=["rpbp"], writes=["bexp"], dsem="bexp")

    stage = xrf[:, 0:3584].rearrange("p (h i q) -> p h i q", h=4, i=14)
    stf = xrf[:, 0:3584].rearrange("p (r q) -> p r q", q=64)

    STK = ["st%d_%d" % (hl, a) for hl in range(4) for a in range(2)]

    XR4 = ["xr0", "xr1", "xr2", "xr3"]

    def tstage(hh):
        P.fence("sp", XR4)
        for hl in range(4):
            h = hh * 4 + hl
            for a in range(2):
                src = bass.AP(tensor=bexp.tensor, offset=(h * 15 + a + 13) * 4096,
                              ap=[[64, 64], [-4096, 14], [1, 64]])
                P.op("sp", lambda e, a=a, hl=hl, src=src: e.dma_start(out=stage[a * 64:(a + 1) * 64, hl], in_=src),
                     reads=["bexp"], writes=["st%d_%d" % (hl, a)], dsem="stage")

    def tdve(hh):
        P.op("dve", lambda e: e.scalar_tensor_tensor(out=stf, in0=stf, scalar=8.0,
                                                     in1=m01[:].unsqueeze(1).broadcast_to([128, 56, 64]),
                                                     op0=ALU.mult, op1=ALU.mult),
             reads=STK + XR4 + ["m01"], writes=STK + XR4)
        P.op("dve", lambda e: e.tensor_tensor(
            out=tab[:, hh * 4:(hh + 1) * 4, 0:14, :], in0=stage,
            in1=mneg[:].unsqueeze(1).unsqueeze(1).broadcast_to([128, 4, 14, 64]), op=ALU.add),
            reads=STK + XR4 + ["mneg"], writes=["tab"])

    def tfinal():
        P.op("dve", lambda e: e.memset(tab[:, :, 14, :], NEG), writes=["tab"])
        P.op("dve", lambda e: e.tensor_copy(out=tab[:, :, 15, :], in_=tab[:, :, 3, :]), reads=["tab"], writes=["tab"])
        P.op("dve", lambda e: e.memset(tab[64:128, :, 15, :], NEG), writes=["tab"])
        P.op("dve", lambda e: e.tensor_copy(out=tab[:, :, 16:18, :], in_=tab[:, :, 10:12, :]), reads=["tab"], writes=["tab"])
        P.op("dve", lambda e: e.memset(tab[0:64, :, 17, :], NEG), writes=["tab"])

    table_steps += [tstep0, tstep1, lambda: tstage(0), lambda: (tdve(0), tstage(1)), lambda: (tdve(1), tfinal())]

    order = []
    for S in seq_lens:
        order += [BK, BV, BP, BK, BV, BP, BK, BV, BP]
        for s in range(S // 512):
            order += [BQ, BG[0], BNP, BG[1], BG[2], BG[3], BWO[0], BWO[1]]
            if 2 * s + 3 < S // 256:
                order += [BK, BV, BP]
            for half in range(2):
                order += [12 + half * 8 + i for i in range(4)]
                order += [16 + half * 8 + i for i in range(4)]
    ws = dict(nload=0, nuse=0, free=list(range(NB)), slot_of={})

    def ws_fill():
        while ws["free"] and ws["nload"] < len(order):
            i = ws["nload"]
            ws["nload"] += 1
            b = order[i]
            slot = ws["free"].pop(0)
            ws["slot_of"][i] = slot
            for (vf, src, key) in block_srcs(b):
                P.op("sp", lambda e, slot=slot, vf=vf, src=src: e.dma_start(out=vf(wring[:, slot, :]), in_=src),
                     reads=[key], writes=["wr%d" % slot], dsem="wr%d" % slot)

    def ws_acquire(b):
        i = ws["nuse"]
        assert order[i] == b, (i, order[i], b)
        assert i < ws["nload"], "weight block not prefetched"
        ws["nuse"] += 1
        slot = ws["slot_of"].pop(i)
        return slot, "wr%d" % slot

    def ws_release(slot):
        ws["free"].append(slot)
        ws_fill()


    def wk8(slot):
        return wring[:, slot, :].rearrange("p (kc n) -> p kc n", kc=8)

    def wk4(slot):
        return wring[:, slot, :].rearrange("p (kc n) -> p kc n", kc=4)

    evac_ctr = [0]

    def evac_copy(out, in_, reads, writes, eng=None):
        if eng is None:
            eng = "act" if evac_ctr[0] % 2 == 0 else "dve"
            evac_ctr[0] += 1
        if eng == "act":
            P.op("act", lambda e: e.copy(out=out, in_=in_), reads=reads, writes=writes)
        else:
            P.op("dve", lambda e: e.tensor_copy(out=out, in_=in_), reads=reads, writes=writes)

    def rms_rstd(src, srckey, col, jslot):
        P.op("act", lambda e: e.activation(out=xs[:, jslot, :], in_=src, func=AF.Square,
                                           accum_out=ssq[:, col:col + 1]),
             reads=[srckey], writes=["ssq%d" % col, "xs%d" % jslot])
        P.op("pool", lambda e: e.tensor_scalar(out=var[:, col:col + 1], in0=ssq[:, col:col + 1], scalar1=1.0 / D,
                                               scalar2=EPS, op0=ALU.mult, op1=ALU.add),
             reads=["ssq%d" % col], writes=["var%d" % col])
        P.op("pool", lambda e: e.tensor_tensor(out=rstd[:, col:col + 1], in0=var[:, col:col + 1], in1=nhalf[:],
                                               op=ALU.pow),
             reads=["var%d" % col, "nhalf"], writes=["rstd%d" % col])

    def norm_prep(src, srckey, col, gb, gkey, xsslot):
        rms_rstd(src, srckey, col, xsslot)
        xsv = xs[:, xsslot, :]
        P.op("dve", lambda e: e.scalar_tensor_tensor(out=xsv, in0=src, scalar=rstd[:, col:col + 1], in1=gb[:],
                                                     op0=ALU.mult, op1=ALU.mult),
             reads=[srckey, "rstd%d" % col, gkey], writes=["xs%d" % xsslot])

    def norm_tr(xsslot, dstT, dstkeys, tcol):
        xsv = xs[:, xsslot, :]
        b = nbank()

        def f(e):
            ins = None
            for c in range(8):
                ins = e.transpose(out=pb16(b, 1024)[:, c * 128:(c + 1) * 128], in_=xsv[:, c * 128:(c + 1) * 128],
                                  identity=ident[:])
            return ins
        P.op("pe", f, reads=["xs%d" % xsslot, "ident"], writes=["ps%d" % b])
        evac_copy(dstT[:, :, tcol:tcol + 128], pb16(b, 1024).rearrange("p (c t) -> p c t", c=8),
                  reads=["ps%d" % b], writes=dstkeys)

    def run_seq(x, y, S):
        J = S // 128
        NU = S // 256
        NS = S // 512

        def a_load(j):
            sl = j % 2
            P.op("sp", lambda e: e.dma_start(out=xa[:, sl, :], in_=x[j * 128:(j + 1) * 128, :]),
                 writes=["xa%d" % sl], dsem="xa%d" % sl)

        def a_norm(j):
            sl = j % 2
            norm_prep(xa[:, sl, :], "xa%d" % sl, sl, g1b, "g1b", sl)
            if not pending_casts:
                for _ in range(2):
                    if late_casts:
                        cast(*late_casts.pop(0))

        def a_tr(j):
            u, tl = j // 2, j % 2
            us = u % 4
            norm_tr(j % 2, xnT, ["xnT%d" % us], us * 256 + tl * 128)

        def a_back(bid, units):
            slot, wkey = ws_acquire(bid)
            W = wk8(slot)
            for u in units:
                us = u % 4
                xu = xnT[:, :, us * 256:(us + 1) * 256]
                if bid == BK:
                    for cp in range(2):
                        b = nbank()

                        def f(e, cp=cp, b=b, xu=xu):
                            ins = None
                            for ci in range(2):
                                c = cp * 2 + ci
                                for k in range(8):
                                    ins = e.matmul(out=pb(b, ci * 256, ci * 256 + 256),
                                                   lhsT=W[:, k, c * 128:(c + 1) * 128], rhs=xu[:, k, :],
                                                   start=(k == 0), stop=(k == 7))
                            return ins
                        P.op("pe", f, reads=[wkey, "xnT%d" % us], writes=["ps%d" % b])
                        evac_copy(kT[:, cp * 2:cp * 2 + 2, us * 256:(us + 1) * 256],
                                  pb(b).rearrange("p (c t) -> p c t", c=2),
                                  reads=["ps%d" % b], writes=["kT%d" % us])
                else:
                    for tl in range(2):
                        c = 2 * u + tl
                        cs_ = c % 8
                        b = nbank()

                        def f(e, tl=tl, b=b, xu=xu):
                            ins = None
                            for k in range(8):
                                ins = e.matmul(out=pb(b), lhsT=xu[:, k, tl * 128:(tl + 1) * 128], rhs=W[:, k, :],
                                               start=(k == 0), stop=(k == 7))
                            return ins
                        P.op("pe", f, reads=[wkey, "xnT%d" % us], writes=["ps%d" % b])
                        if bid == BV:
                            evac_copy(vtok[:, cs_, :, 0:64], pb(b).rearrange("p (h d) -> p h d", h=8),
                                      reads=["ps%d" % b], writes=["vtok%d" % cs_])
                        else:
                            evac_copy(ptok[:, cs_, :], pb(b), reads=["ps%d" % b], writes=["ptok%d" % cs_])
            ws_release(slot)

        def attention_tile(j, tl, prev_finish):
            if j == 0:
                dl = [(d, 6 - 2 * d) for d in (0, 1, 2, 3)]
            elif j == 1:
                dl = [(d, 6 - 2 * d) for d in (-1, 0, 1, 2)]
            elif j == J - 2:
                dl = [(d, 6 - 2 * d) for d in (-2, -1, 0, 1)]
            elif j == J - 1:
                dl = [(d, 6 - 2 * d) for d in (-3, -2, -1, 0)]
            else:
                dl = [(-2, 16), (-1, 8), (0, 6), (1, 4), (2, 14)]
            combos = [(m, d, idx) for m in range(4) for (d, idx) in dl]
            ob = [nbank(), nbank()]
            reserved.update(ob)
            where = {}
            pv_done = set()
            LAG = 2
            ngroups = (len(combos) + 1) // 2
            last_group = {}
            for ci, (m, d, idx) in enumerate(combos):
                last_group[m] = ci // 2

            def emit_pv(m):
                for hh in range(2):
                    h = 2 * m + hh
                    bo = ob[h // 4]
                    ocol = (h % 4) * 65

                    def fpv(e, h=h, hh=hh, bo=bo, ocol=ocol, wh=dict(where)):
                        ins = None
                        for di, (d, _) in enumerate(dl):
                            ps_, half = wh[(h // 2, d)]
                            c = j + d
                            base = half * 256 + hh * 128
                            ins = e.matmul(out=pb(bo, ocol, ocol + 65), lhsT=pTs[:, ps_, base:base + 128],
                                           rhs=vtok[:, c % 8, h, :], start=(di == 0), stop=(di == len(dl) - 1))
                        return ins
                    ptkeys = sorted(set("pTs%d" % where[(m, d)][0] for (d, _) in dl))
                    vkeys = ["vtok%d" % ((j + d) % 8) for (d, _) in dl]
                    P.op("pe", fpv, reads=ptkeys + vkeys, writes=["ps%d" % bo])

            for g in range(ngroups):
                grp = [(ci, combos[ci]) for ci in range(g * 2, min(g * 2 + 2, len(combos)))]
                b = nbank()
                pslot = pt_ctr[0] % NPT
                pt_ctr[0] += 1
                via_pe = {ci: (ci % 5) != 2 for ci, _ in grp}

                def f(e, grp=grp, b=b, via_pe=via_pe):
                    ins = None
                    for qi, (ci, (m, d, idx)) in enumerate(grp):
                        kcol = ((j + d) % 8) * 128
                        o = pb(b, qi * 256, qi * 256 + 256)
                        if via_pe[ci]:
                            e.matmul(out=o, lhsT=kT[:, m, kcol:kcol + 128], rhs=qbd[:, m, tl, :, :],
                                     start=True, stop=False)
                            ins = e.matmul(out=o, lhsT=ident[:],
                                           rhs=tab[:, 2 * m:2 * m + 2, idx:idx + 2, :].rearrange("p h i q -> p h (i q)"),
                                           start=False, stop=True)
                        else:
                            ins = e.matmul(out=o, lhsT=kT[:, m, kcol:kcol + 128], rhs=qbd[:, m, tl, :, :],
                                           start=True, stop=True)
                    return ins
                kkeys = sorted(set("kT%d" % (((j + d) // 2) % 4) for _, (m, d, idx) in grp))
                qkeys = sorted(set("qbd%d_%d" % (m, hh) for _, (m, d, idx) in grp for hh in range(2)))
                P.op("pe", f, reads=qkeys + kkeys + ["tab", "ident"], writes=["ps%d" % b])
                for qi, (ci, (m, d, idx)) in enumerate(grp):
                    if not via_pe[ci]:
                        pv_ = pb(b, qi * 256, qi * 256 + 256).rearrange("p (h q) -> p h q", h=2)
                        P.op("dve", lambda e, pv_=pv_, m=m, idx=idx: e.tensor_tensor(
                            out=pv_, in0=pv_,
                            in1=tab[:, 2 * m:2 * m + 2, idx:idx + 2, :].rearrange("p h i q -> p h (i q)"), op=ALU.add),
                            reads=["ps%d" % b, "tab"], writes=["ps%d" % b])
                    where[(m, d)] = (pslot, qi)
                n = len(grp) * 256
                P.op("act", lambda e, b=b, pslot=pslot, n=n: e.activation(out=pTs[:, pslot, 0:n], in_=pb(b, 0, n),
                                                                          func=AF.Exp, scale=0.125),
                     reads=["ps%d" % b], writes=["pTs%d" % pslot])
                if g == 5 and prev_finish is not None:
                    prev_finish()
                    prev_finish = None
                for m in range(4):
                    if m not in pv_done and last_group[m] <= g - LAG:
                        pv_done.add(m)
                        emit_pv(m)
            if prev_finish is not None:
                prev_finish()
            for m in range(4):
                if m not in pv_done:
                    pv_done.add(m)
                    emit_pv(m)

            def finish():
                osl = tl % 2
                for hb in range(2):
                    bo = ob[hb]
                    ov = pb(bo, 0, 260).rearrange("p (h d) -> p h d", h=4)
                    P.op("dve", lambda e, ov=ov, hb=hb: e.reciprocal(out=rec[:, osl, hb * 4:(hb + 1) * 4].unsqueeze(2),
                                                                     in_=ov[:, :, 64:65]),
                         reads=["ps%d" % bo], writes=["rec%d_%d" % (osl, hb)])
                    P.op("dve", lambda e, ov=ov, hb=hb: e.tensor_tensor(
                        out=otok[:, osl, hb * 256:(hb + 1) * 256].rearrange("p (h d) -> p h d", h=4),
                        in0=ov[:, :, 0:64],
                        in1=rec[:, osl, hb * 4:(hb + 1) * 4].unsqueeze(2).broadcast_to([128, 4, 64]),
                        op=ALU.mult),
                        reads=["ps%d" % bo, "rec%d_%d" % (osl, hb)], writes=["otok%d_%d" % (osl, hb)])
                reserved.difference_update(ob)
                bt = nbank()

                def ftr(e):
                    ins = None
                    for m in range(4):
                        ins = e.transpose(out=pb16(bt, 512)[:, m * 128:(m + 1) * 128],
                                          in_=otok[:, osl, m * 128:(m + 1) * 128], identity=ident[:])
                    return ins
                P.op("pe", ftr, reads=["otok%d_0" % osl, "otok%d_1" % osl, "ident"], writes=["ps%d" % bt])
                evac_copy(oT[:, :, tl * 128:(tl + 1) * 128], pb16(bt, 512).rearrange("p (c t) -> p c t", c=4),
                          reads=["ps%d" % bt], writes=["oT"] + HID_OT)
            return finish

        def stage_b(s):
            XN = xnT[:, :, (s % 2) * 512:(s % 2) * 512 + 512]
            xnkeys = ["xnT%d" % ((2 * s) % 4), "xnT%d" % ((2 * s + 1) % 4)]
            nxt = [jj for jj in range(4 * s + 6, 4 * s + 10) if jj < J]
            nunits = [u for u in (2 * s + 3, 2 * s + 4) if u < NU]
            P.op("sp", lambda e: e.dma_start(out=xn2T[:].rearrange("p c t -> p (c t)").rearrange("p (kc n) -> p kc n", kc=4),
                                             in_=wb_pp.rearrange("(kc p) n -> p kc n", p=128)),
                 reads=["wb_pp"], writes=["xn2T"], dsem="ppx")
            for jj in nxt[0:2]:
                a_load(jj)
            for tl in range(4):
                j = 4 * s + tl
                P.op("sp", lambda e, j=j, tl=tl: e.dma_start(out=xr[:, tl, :], in_=x[j * 128:(j + 1) * 128, :]),
                     writes=["xr%d" % tl], dsem="xr%d" % tl)
            slot, wkey = ws_acquire(BQ)
            W = wk8(slot)
            for c in range(4):
                b = nbank()

                def f(e, c=c, b=b, W=W):
                    ins = None
                    for k in range(8):
                        ins = e.matmul(out=pb(b), lhsT=W[:, k, c * 128:(c + 1) * 128], rhs=XN[:, k, :],
                                       start=(k == 0), stop=(k == 7))
                    return ins
                P.op("pe", f, reads=[wkey] + xnkeys, writes=["ps%d" % b])
                for hh in range(2):
                    evac_copy(qbd[hh * 64:(hh + 1) * 64, c, :, hh, :],
                              pb(b)[hh * 64:(hh + 1) * 64, :].rearrange("p (t q) -> p t q", t=4),
                              reads=["ps%d" % b], writes=["qbd%d_%d" % (c, hh)])
            ws_release(slot)
            for tl in range(4):
                j = 4 * s + tl
                b = nbank()

                def fp(e, j=j, b=b):
                    ins = None
                    for g in range(4):
                        srcs = []
                        if j > 0:
                            srcs.append((j - 1, g * 5 + 0))
                        srcs.append((j, g * 5 + (3 if j == 0 else (4 if j == J - 1 else 1))))
                        if j < J - 1:
                            srcs.append((j + 1, g * 5 + 2))
                        for si, (c, v) in enumerate(srcs):
                            ins = e.matmul(out=pb(b, g * 128, (g + 1) * 128),
                                           lhsT=ptok[:, c % 8, g * 128:(g + 1) * 128], rhs=amat[:, v, :],
                                           start=(si == 0), stop=(si == len(srcs) - 1))
                    return ins
                pkeys = ["ptok%d" % (c % 8) for c in (j - 1, j, j + 1) if 0 <= c < J]
                P.op("pe", fp, reads=pkeys + ["amat"], writes=["ps%d" % b])
                evac_copy(pooledT[:, :, tl * 128:(tl + 1) * 128], pb(b).rearrange("p (g t) -> p g t", g=4),
                          reads=["ps%d" % b], writes=["pooledT"] + HID_PO)
            def emit_mixed():
                for g in range(4):
                    b = nbank()
                    P.op("pe", lambda e, g=g, b=b: e.matmul(out=pb(b), lhsT=wgrp[:, g, :], rhs=pooledT[:, g, :],
                                                            start=True, stop=True),
                         reads=["wgrp", "pooledT"], writes=["ps%d" % b])
                    P.op("dve", lambda e, g=g, b=b: e.tensor_scalar(out=mixedT[:, g, :], in0=pb(b), scalar1=psT[:, g:g + 1],
                                                                    scalar2=None, op0=ALU.mult),
                         reads=["ps%d" % b, "psT"], writes=["mixedT"] + HID_MX)
            fin = None
            for tl in range(4):
                fin = attention_tile(4 * s + tl, tl, fin)
                if tl == 0:
                    emit_mixed()
            for jj in nxt[0:2]:
                a_norm(jj)
            for jj in nxt[2:4]:
                a_load(jj)
            fin_pending = [fin]
            gslot, gkey = ws_acquire(BG[0])
            npslot, npkey = ws_acquire(BNP)
            ppkey = "xn2T"
            WPP, WNP = xn2T[:].rearrange("p c t -> p (c t)").rearrange("p (kc n) -> p kc n", kc=4), wk4(npslot)
            for bi in range(4):
                if bi > 0:
                    gslot, gkey = ws_acquire(BG[bi])
                WG = wk8(gslot)
                for i in range(2):
                    m = 2 * bi + i
                    gb_ = m % 2
                    for br in range(2):
                        b = nbank()
                        col0 = br * 256 + i * 128

                        def fg(e, b=b, col0=col0, WG=WG):
                            ins = None
                            for k in range(8):
                                ins = e.matmul(out=pb(b), lhsT=WG[:, k, col0:col0 + 128], rhs=XN[:, k, :],
                                               start=(k == 0), stop=(k == 7))
                            return ins
                        P.op("pe", fg, reads=[gkey] + xnkeys, writes=["ps%d" % b])
                        bcol = br * 8 + m
                        P.op("act", lambda e, b=b, br=br, gb_=gb_, bcol=bcol: e.activation(
                            out=tg[:, br, gb_, :], in_=pb(b), func=AF.Tanh, bias=hbT[:, bcol:bcol + 1], scale=0.5),
                            reads=["ps%d" % b, "hbT"], writes=["tg%d_%d" % (br, gb_)])
                        if br == 1 and fin_pending:
                            fin_pending.pop()()
                        b2 = nbank()
                        Wp = WPP if br == 0 else WNP
                        src = mixedT if br == 0 else oT

                        def fpr(e, b2=b2, Wp=Wp, src=src, m=m):
                            ins = None
                            for k in range(4):
                                ins = e.matmul(out=pb(b2), lhsT=Wp[:, k, m * 128:(m + 1) * 128], rhs=src[:, k, :],
                                               start=(k == 0), stop=(k == 3))
                            return ins
                        P.op("pe", fpr, reads=[ppkey if br == 0 else npkey, "mixedT" if br == 0 else "oT"],
                             writes=["ps%d" % b2])
                        P.op("dve", lambda e, b2=b2, br=br, gb_=gb_: e.scalar_tensor_tensor(
                            out=ug[:, br, :], in0=tg[:, br, gb_, :], scalar=1.0, in1=pb(b2),
                            op0=ALU.add, op1=ALU.mult),
                            reads=["ps%d" % b2, "tg%d_%d" % (br, gb_)], writes=["ug%d" % br])
                    P.op("dve", lambda e, m=m: e.tensor_tensor(out=mT[:, m, :], in0=ug[:, 0, :],
                                                               in1=ug[:, 1, :], op=ALU.add),
                         reads=["ug0", "ug1"], writes=["mT"])
                ws_release(gslot)
                if bi == 1:
                    for jj in nxt[0:2]:
                        a_tr(jj)
                    for jj in nxt[2:4]:
                        a_norm(jj)
            ws_release(npslot)
            for hf in range(2):
                slot, wkey = ws_acquire(BWO[hf])
                W = wk8(slot)
                for tl in range(4):
                    b = nbank()

                    def fo(e, b=b, tl=tl, W=W):
                        ins = None
                        for k in range(8):
                            ins = e.matmul(out=pb(b), lhsT=mT[:, k, tl * 128:(tl + 1) * 128], rhs=W[:, k, :],
                                           start=(k == 0), stop=(k == 7))
                        return ins
                    P.op("pe", fo, reads=[wkey, "mT"], writes=["ps%d" % b])
                    xv = xr[:, tl, hf * 512:(hf + 1) * 512]
                    P.op("dve", lambda e, b=b, xv=xv: e.scalar_tensor_tensor(out=xv, in0=pb(b), scalar=0.5, in1=xv,
                                                                             op0=ALU.mult, op1=ALU.add),
                         reads=["ps%d" % b, "xr%d" % tl], writes=["xr%d" % tl])
                ws_release(slot)
                if hf == 0:
                    for jj in nxt[2:4]:
                        a_tr(jj)
            def n_prep(tl):
                norm_prep(xr[:, tl, :], "xr%d" % tl, 2 + tl, g2b, "g2b", tl % 2)

            def n_tr(tl):
                norm_tr(tl % 2, xn2T, ["xn2T"], tl * 128)
            while late_casts:
                cast(*late_casts.pop(0))
            n_prep(0)
            n_prep(1)
            if nunits:
                a_back(BK, nunits)
            n_tr(0)
            n_tr(1)
            n_prep(2)
            n_prep(3)
            if nunits:
                a_back(BV, nunits)
            n_tr(2)
            n_tr(3)
            if nunits:
                a_back(BP, nunits)
            for half in range(2):
                for fb in range(4):
                    slot, wkey = ws_acquire(12 + half * 8 + fb)
                    W = wk8(slot)
                    for c4 in range(4):
                        b = nbank()
                        hc = fb * 4 + c4

                        def f1(e, b=b, c4=c4, W=W):
                            ins = None
                            for k in range(8):
                                ins = e.matmul(out=pb(b), lhsT=W[:, k, c4 * 128:(c4 + 1) * 128], rhs=xn2T[:, k, :],
                                               start=(k == 0), stop=(k == 7))
                            return ins
                        P.op("pe", f1, reads=[wkey, "xn2T"], writes=["ps%d" % b])
                        rs_ = 0
                        P.op("act", lambda e, b=b, rs_=rs_: e.activation(out=rl[:, rs_, :], in_=pb(b), func=AF.Relu),
                             reads=["ps%d" % b], writes=["rl%d" % rs_])
                        P.op("dve", lambda e, rs_=rs_, hc=hc: e.tensor_tensor(out=hidT[:, hc, :], in0=rl[:, rs_, :],
                                                                              in1=rl[:, rs_, :], op=ALU.mult),
                             reads=["rl%d" % rs_],
                             writes=["hid%d" % hc] + (["oT"] if hc < 4 else ["pooledT"] if hc < 8 else ["mixedT"] if hc < 12 else []))
                    ws_release(slot)
                for ch in range(2):
                    banks = [nbank() for _ in range(4)]
                    for kg in range(2):
                        slot, wkey = ws_acquire(16 + half * 8 + ch * 2 + kg)
                        W = wk8(slot)

                        def f2(e, kg=kg, W=W, banks=banks):
                            ins = None
                            for k in range(8):
                                for tl in range(4):
                                    ins = e.matmul(out=pb(banks[tl]), lhsT=hidT[:, kg * 8 + k, tl * 128:(tl + 1) * 128],
                                                   rhs=W[:, k, :], start=(kg == 0 and k == 0),
                                                   stop=(kg == 1 and k == 7))
                            return ins
                        P.op("pe", f2, reads=[wkey] + ["hid%d" % (kg * 8 + k) for k in range(8)],
                             writes=["ps%d" % b for b in banks])
                        ws_release(slot)
                    for tl in range(4):
                        xv = xr[:, tl, ch * 512:(ch + 1) * 512]
                        P.op("dve", lambda e, b=banks[tl], xv=xv: e.tensor_tensor(out=xv, in0=pb(b), in1=xv, op=ALU.add),
                             reads=["ps%d" % banks[tl], "xr%d" % tl], writes=["xr%d" % tl])
            for tl in range(4):
                j = 4 * s + tl
                rms_rstd(xr[:, tl, :], "xr%d" % tl, 2 + tl, tl % 2)
                P.op("dve", lambda e, tl=tl: e.scalar_tensor_tensor(out=xr[:, tl, :], in0=xr[:, tl, :],
                                                                    scalar=rstd[:, 2 + tl:3 + tl], in1=nfb[:],
                                                                    op0=ALU.mult, op1=ALU.mult),
                     reads=["xr%d" % tl, "rstd%d" % (2 + tl), "nfb"], writes=["xr%d" % tl])
                P.op("sp", lambda e, j=j, tl=tl: e.dma_start(out=y[j * 128:(j + 1) * 128, :], in_=xr[:, tl, :]),
                     reads=["xr%d" % tl], writes=["yout%d" % tl], dsem="yo%d" % tl)

        def pro_extras(jj):
            for c_ in pending_casts.pop(jj, []):
                cast(*c_)
            if table_steps and jj >= 2:
                table_steps.pop(0)()
        a_load(0)
        a_load(1)
        if table_steps:
            traw()
        ws_fill()
        a_norm(0)
        pro_extras(0)
        a_load(2)
        a_norm(1)
        pro_extras(1)
        a_load(3)
        a_tr(0)
        a_norm(2)
        pro_extras(2)
        a_load(4)
        a_tr(1)
        a_norm(3)
        pro_extras(3)
        a_load(5)
        for bid in (BK, BV, BP):
            a_back(bid, [0])
        a_tr(2)
        a_norm(4)
        pro_extras(4)
        a_tr(3)
        a_norm(5)
        pro_extras(5)
        for bid in (BK, BV, BP):
            a_back(bid, [1])
        a_tr(4)
        a_tr(5)
        for bid in (BK, BV, BP):
            a_back(bid, [2])
        while table_steps:
            table_steps.pop(0)()
        for s in range(NS):
            stage_b(s)

    for i, S in enumerate(seq_lens):
        run_seq(xin[i], yout[i], S)
    assert ws["nuse"] == len(order), (ws["nuse"], len(order))
    P.op("sp", lambda e: None, reads=["yout%d" % t for t in range(4)])

    with nc.Block() as block:
        @block.tensor
        def _(e):
            P.replay("pe", e)

        @block.scalar
        def _(e):
            P.replay("act", e)

        @block.vector
        def _(e):
            P.replay("dve", e)

        @block.gpsimd
        def _(e):
            P.replay("pool", e)

        @block.sync
        def _(e):
            P.replay("sp", e)
    P.stack.close()
    return nc


_CONSTS = None


def _consts():
    global _CONSTS
    if _CONSTS is None:
        m01, mneg = _col_masks()
        _CONSTS = dict(c_ident=np.eye(128, dtype=np.float32), c_amat=_pool_mats(), c_m01=m01, c_mneg=mneg)
    return _CONSTS


def make_in_maps(xs_list, weights):
    c = _consts()
    maps = []
    for xs_ in xs_list:
        m = dict(c)
        for i, x in enumerate(xs_):
            m["x%d" % i] = np.ascontiguousarray(x, dtype=np.float32)
        m.update(weights)
        maps.append(m)
    return maps


def prep_weights(norm_mix, w_in, b_gate, w_pool_grp, pool_scale, w_pool_proj, rpb, w_na_proj, w_out, norm_mlp,
                 w_ff1, w_ff2, norm_final):
    f = lambda a: np.ascontiguousarray(np.asarray(a, dtype=np.float32))
    return dict(norm_mix=f(norm_mix[0]), w_in=f(w_in[0]), b_gate=f(b_gate[0]), w_pool_grp=f(w_pool_grp[0]),
                pool_scale=f(pool_scale[0]), w_pool_proj=f(w_pool_proj[0]), rpb=f(rpb[0]), w_na_proj=f(w_na_proj[0]),
                w_out=f(w_out[0]), norm_mlp=f(norm_mlp[0]), w_ff1=f(w_ff1[0]), w_ff2=f(w_ff2[0]),
                norm_final=f(norm_final))


def kernel(x_prompt, x_sample, norm_mix, w_in, b_gate, w_pool_grp, pool_scale, w_pool_proj, rpb, w_na_proj, w_out,
           norm_mlp, w_ff1, w_ff2, norm_final):
    x_prompt = np.asarray(x_prompt)
    x_sample = np.asarray(x_sample)
    ncores = 8
    SP_, SS_ = x_prompt.shape[1], x_sample.shape[1]
    nc = build_nc((SP_, SS_))
    weights = prep_weights(norm_mix, w_in, b_gate, w_pool_grp, pool_scale, w_pool_proj, rpb, w_na_proj, w_out,
                           norm_mlp, w_ff1, w_ff2, norm_final)
    in_maps = make_in_maps([[x_prompt[i], x_sample[i]] for i in range(ncores)], weights)
    res = run_bass_kernel_spmd(nc, in_maps, core_ids=list(range(ncores)))
    yp = np.stack([np.asarray(r["y0"], dtype=np.float32) for r in res.results], axis=0)
    ys = np.stack([np.asarray(r["y1"], dtype=np.float32) for r in res.results], axis=0)
    return (yp, ys)
```
